# Optimizing a Trainium2 kernel written in Bass

```python
import jax
import jax.numpy as jnp
from jax import lax
import numpy as np

D_MODEL = 1024
BATCH = 4
SEQ = 8192
DEPTH = 1
DEC_BATCH = 1
DEC_SEQ = 16384
PAST_LEN = 128

D_A = D_MODEL
D_B = D_MODEL
D_MIX = D_A + D_B
H_A = 8
HD_A = D_A // H_A
H_B = 8
HD_B = D_B // H_B
GMLP_CHUNK = 128
HGRN_CHUNK = 64
D_IN = 3 * D_A + 5 * D_B
EPS = 1e-6

kernel_name = 'bidir_gmlp_hgrn2_hybrid'


def rmsnorm(x, g):
    xf = x.astype(jnp.float32)
    y = xf * lax.rsqrt(jnp.mean(xf * xf, axis=-1, keepdims=True) + EPS)
    return (y * g.astype(jnp.float32)).astype(x.dtype)


def layernorm(x, g, b):
    xf = x.astype(jnp.float32)
    xc = xf - jnp.mean(xf, axis=-1, keepdims=True)
    y = xc * lax.rsqrt(jnp.mean(xc * xc, axis=-1, keepdims=True) + EPS)
    return (y * g.astype(jnp.float32) + b.astype(jnp.float32)).astype(x.dtype)


def gmlp_spatial_gating(u, v, ln_g, ln_b, w_s, b_s):
    bsz, seq, _ = u.shape
    vn = layernorm(v, ln_g, ln_b).reshape(bsz, seq // GMLP_CHUNK, GMLP_CHUNK, H_A, HD_A)
    s = jnp.einsum('hts,bnshc->bnthc', w_s, vn) + b_s.T[:, :, None]
    return u * s.reshape(bsz, seq, D_A)


def gla_chunk_scan(q, k, g, v):
    bsz, seq, nh, dk = q.shape
    dv = v.shape[-1]
    n_chunks = seq // HGRN_CHUNK

    def to_chunks(t):
        return t.reshape(bsz, n_chunks, HGRN_CHUNK, nh, t.shape[-1]).transpose(1, 0, 3, 2, 4)

    incl = jnp.tril(jnp.ones((HGRN_CHUNK, HGRN_CHUNK), dtype=bool))[:, :, None]

    def step(state, chunk):
        qc, kc, gc, vc = chunk
        b = jnp.cumsum(gc, axis=2)
        o_inter = jnp.einsum('bhtk,bhkv->bhtv', qc * jnp.exp(b), state)
        diff = b[:, :, :, None, :] - b[:, :, None, :, :]
        decay = jnp.exp(jnp.where(incl, diff, -jnp.inf))
        scores = jnp.einsum('bhtsk,bhsk->bhts', qc[:, :, :, None, :] * decay, kc)
        o = o_inter + jnp.einsum('bhts,bhsv->bhtv', scores, vc)
        b_end = b[:, :, -1:, :]
        new_state = (jnp.exp(b_end[:, :, 0, :])[..., None] * state
                     + jnp.einsum('bhsk,bhsv->bhkv', kc * jnp.exp(b_end - b), vc))
        return new_state, o

    s0 = jnp.zeros((bsz, nh, dk, dv), jnp.float32)
    _, o = lax.scan(step, s0, (to_chunks(q), to_chunks(k), to_chunks(g), to_chunks(v)))
    return o.transpose(1, 0, 3, 2, 4).reshape(bsz, seq, nh, dv)


def hgrn2_bidirectional(q, f_fwd, f_bwd, i, lb_fwd, lb_bwd, gn_g, z):
    bsz, seq, _ = q.shape

    def heads(t):
        return t.astype(jnp.float32).reshape(bsz, seq, H_B, HD_B)

    qh = heads(jax.nn.silu(q))
    ih = heads(i)

    def one_direction(f_logit, lb, reverse):
        fg = lb + (1.0 - lb) * jax.nn.sigmoid(f_logit.astype(jnp.float32))
        args = (qh, heads(1.0 - fg), heads(jnp.log(fg)), ih)
        if reverse:
            args = tuple(jnp.flip(t, axis=1) for t in args)
        o = gla_chunk_scan(*args)
        return jnp.flip(o, axis=1) if reverse else o

    o = one_direction(f_fwd, lb_fwd, False) + one_direction(f_bwd, lb_bwd, True)
    o = o * lax.rsqrt(jnp.mean(o * o, axis=-1, keepdims=True) + EPS)
    o = o.reshape(bsz, seq, D_B) * gn_g.astype(jnp.float32)
    return (o * jax.nn.silu(z.astype(jnp.float32))).astype(z.dtype)


def encoder_trunk(x, norm_g, w_in, ln_v_g, ln_v_b, w_s, b_s, lb_params, gn_g, w_out, final_g):
    lower_bounds = jnp.cumsum(jax.nn.softmax(lb_params.astype(jnp.float32), axis=1), axis=1)
    cuts = [D_A, 2 * D_A, 3 * D_A, 3 * D_A + D_B, 3 * D_A + 2 * D_B,
            3 * D_A + 3 * D_B, 3 * D_A + 4 * D_B]
    for layer in range(DEPTH):
        h = rmsnorm(x, norm_g[layer])
        proj = jnp.einsum('bld,de->ble', h, w_in[layer])
        u_a, v_a, z_a, q_b, f_fwd, f_bwd, i_b, z_b = jnp.split(proj, cuts, axis=-1)
        out_a = gmlp_spatial_gating(u_a, v_a, ln_v_g[layer], ln_v_b[layer],
                                    w_s[layer], b_s[layer]) * jax.nn.silu(z_a)
        out_b = hgrn2_bidirectional(q_b, f_fwd, f_bwd, i_b, lower_bounds[0, layer],
                                    lower_bounds[1, layer], gn_g[layer], z_b)
        mixed = jnp.concatenate([out_a, out_b], axis=-1)
        x = x + jnp.einsum('ble,ed->bld', mixed, w_out[layer]).astype(x.dtype)
    return rmsnorm(x, final_g)


def setup_inputs(seed: int = 0) -> dict:
    key = jax.random.key(seed)
    ks = jax.random.split(key, 12)
    f32 = jnp.float32

    def nrm(k, shape, scale):
        return scale * jax.random.normal(k, shape, f32)

    return {
        'x_prompt': jax.random.normal(ks[0], (BATCH, SEQ, D_MODEL), f32),
        'x_sample': jax.random.normal(ks[1], (DEC_BATCH, DEC_SEQ, D_MODEL), f32),
        'norm_g': 1.0 + nrm(ks[2], (DEPTH, D_MODEL), 0.02),
        'w_in': nrm(ks[3], (DEPTH, D_MODEL, D_IN), D_MODEL ** -0.5),
        'ln_v_g': 1.0 + nrm(ks[4], (DEPTH, D_A), 0.02),
        'ln_v_b': nrm(ks[5], (DEPTH, D_A), 0.02),
        'w_s': nrm(ks[6], (DEPTH, H_A, GMLP_CHUNK, GMLP_CHUNK), GMLP_CHUNK ** -0.5),
        'b_s': 1.0 + nrm(ks[7], (DEPTH, H_A, GMLP_CHUNK), 0.02),
        'lb_params': nrm(ks[8], (2, DEPTH + 1, D_B), 0.1),
        'gn_g': 1.0 + nrm(ks[9], (DEPTH, D_B), 0.02),
        'w_out': nrm(ks[10], (DEPTH, D_MIX, D_MODEL), D_MIX ** -0.5),
        'final_g': 1.0 + nrm(ks[11], (D_MODEL,), 0.02),
    }


def reference(x_prompt, x_sample, norm_g, w_in, ln_v_g, ln_v_b, w_s, b_s, lb_params, gn_g, w_out, final_g):
    y_prompt = encoder_trunk(x_prompt, norm_g, w_in, ln_v_g, ln_v_b, w_s, b_s,
                             lb_params, gn_g, w_out, final_g)
    y_sample = encoder_trunk(x_sample, norm_g, w_in, ln_v_g, ln_v_b, w_s, b_s,
                             lb_params, gn_g, w_out, final_g)
    return (y_prompt, y_sample)
```

```python
import numpy as np
import ml_dtypes
from contextlib import ExitStack

import concourse.bass as bass
import concourse.mybir as mybir
from concourse.bass_utils import run_bass_kernel_spmd

F32 = mybir.dt.float32
BF16 = mybir.dt.bfloat16
U8 = mybir.dt.uint8
AF = mybir.ActivationFunctionType
ALU = mybir.AluOpType

NCORES = 8
PART_ORDER = ['zb', 'uz', 'va', 'q', 'i', 'f']
ALIAS_1B = True
NTH = 1
NWB = 3
NSZ = 2
NXH = 2
USE_BARRIERS = False
MULT_ENG = ['dve', 'dve']
D = 1024
NH = 8
T = 512
HB = 128
TT = T + 2 * HB
NBLK = TT // 128
NMB = T // 128
NSEG = 12
FL = T + HB
NCH = FL // 64
DIN = 8192
EPS = 1e-6

DEBUG = False


class Buf:
    __slots__ = ("name", "w", "r", "rng", "partners")

    def __init__(self, name):
        self.name = name
        self.w = None
        self.r = set()
        self.rng = None
        self.partners = None


class _RecIns:
    def then_inc(self, *a, **k):
        return self


class _Rec:
    def __init__(self, eng):
        self.eng = eng
        self.dur = 0.0
        self.tset = None
        self.bytes = 0

    def __getattr__(self, name):
        def call(*a, **k):
            out = k.get("out", a[0] if a else None)
            try:
                n = out.free_size()
            except Exception:
                n = 512
            e = self.eng
            if e == "pe":
                if name == "transpose":
                    self.dur += 0.07
                else:
                    f = 4.0 if k.get("lhsT").dtype == F32 else 1.0
                    self.dur += max(0.035, f * n / 2300.0)
            elif e == "act":
                self.dur += 0.25 + n / 1200.0 + (0.1 if k.get("accum_out") is not None else 0.0)
                fn_ = k.get("func")
                if fn_ in (AF.Silu, AF.Tanh):
                    self.tset = "A"
                elif fn_ in (AF.Ln, AF.Exp):
                    self.tset = "B"
            elif e == "dve":
                if name == "tensor_tensor_scan":
                    self.dur += 0.16 + 2.0 * n / 960.0
                else:
                    self.dur += 0.16 + n / 960.0
            elif e == "pool":
                if name == "dma_start":
                    self.dur += 0.7
                    self.bytes += out.nbytes()
                else:
                    self.dur += 0.2 + n / 480.0
            else:
                self.dur += 0.45
                try:
                    self.bytes += out.nbytes()
                except Exception:
                    pass
            return _RecIns()
        return call


class Prog:
    ENGS = ["pe", "act", "dve", "pool", "sp"]
    SCHED = True
    WIN = 0.25
    TPEN = 0.0

    def __init__(self, nc, stack):
        self.nc = nc
        self.stack = stack
        self.sems = {n: stack.enter_context(nc.semaphore("s_" + n)) for n in self.ENGS}
        self.dsems = []
        self.bufs = {}
        self.stopped = False
        self.ops = []
        self.regions = [[]]

    def buf(self, *key):
        b = self.bufs.get(key)
        if b is None:
            b = Buf(key)
            self.bufs[key] = b
        return b

    def dma_sem(self, name):
        d = dict(sem=self.stack.enter_context(self.nc.semaphore("d_" + name)), cnt=0, name=name)
        self.dsems.append(d)
        return d

    def op(self, eng, fn, reads=(), writes=(), dma=None):
        if self.stopped:
            return None
        oid = len(self.ops)
        deps = set()
        for b in reads:
            if b.w is not None:
                deps.add(b.w)
            for p in self._partners(b):
                if p.w is not None:
                    deps.add(p.w)
        for b in writes:
            if b.w is not None:
                deps.add(b.w)
            deps |= b.r
            for p in self._partners(b):
                if p.w is not None:
                    deps.add(p.w)
                deps |= p.r
        rec = _Rec(eng)
        fn(rec)
        lat = rec.dur
        if dma is not None:
            lat = 2.0 + rec.bytes / 300e3
        self.ops.append(dict(id=oid, eng=eng, fn=fn, deps=deps, dma=dma, dur=rec.dur, lat=lat, tset=rec.tset,
                             region=len(self.regions) - 1))
        self.regions[-1].append(oid)
        for b in reads:
            b.r.add(oid)
        for b in writes:
            b.w = oid
            b.r = set()
        return oid

    def barrier(self):
        if self.regions[-1]:
            self.regions.append([])

    def set_range(self, b, parent, lo, hi):
        b.rng = (parent, lo, hi)
        self.ranged = getattr(self, "ranged", [])
        self.ranged.append(b)
        for x in self.ranged:
            x.partners = None

    def _partners(self, b):
        if b.rng is None:
            return ()
        if b.partners is None:
            pa, lo, hi = b.rng
            b.partners = [x for x in self.ranged if x is not b and x.rng[0] != pa and x.rng[1] < hi and lo < x.rng[2]]
        return b.partners

    def handoff(self, src, dst):
        acc = set()
        for b in src:
            if b.w is not None:
                acc.add(b.w)
            acc |= b.r
        for b in dst:
            b.r |= acc

    def _schedule(self, region):
        ops = self.ops
        rset = set(region)
        order = {n: [] for n in self.ENGS}
        if not self.SCHED:
            for i in region:
                order[ops[i]["eng"]].append(i)
            return order
        succ = {i: [] for i in region}
        ndeps = {}
        for i in region:
            d = [j for j in ops[i]["deps"] if j in rset]
            ops[i]["rdeps"] = d
            ndeps[i] = len(d)
            for j in d:
                succ[j].append(i)
        cp = {}
        for i in reversed(region):
            m = 0.0
            for k in succ[i]:
                if cp[k] > m:
                    m = cp[k]
            cp[i] = ops[i]["lat"] + m
        free = {n: 0.0 for n in self.ENGS}
        fin = {}
        ready = [i for i in region if ndeps[i] == 0]
        cur_set = None
        nleft = len(region)
        WIN = self.WIN
        while nleft:
            cands = []
            mn = None
            for i in ready:
                o = ops[i]
                st = free[o["eng"]]
                for j in o["rdeps"]:
                    t = fin[j] + 0.15
                    if t > st:
                        st = t
                pen = 0.0
                if o["eng"] == "act" and o["tset"] is not None and cur_set is not None and o["tset"] != cur_set:
                    pen = self.TPEN
                cands.append((st + pen, i, st))
                if mn is None or st + pen < mn:
                    mn = st + pen
            best = None
            bkey = None
            for (sp_, i, st) in cands:
                if sp_ <= mn + WIN:
                    key = (-cp[i], sp_, i)
                    if bkey is None or key < bkey:
                        bkey = key
                        best = (i, st)
            i, st = best
            o = ops[i]
            if o["eng"] == "act" and o["tset"] is not None:
                if cur_set is not None and o["tset"] != cur_set:
                    st += 1.3
                cur_set = o["tset"]
            free[o["eng"]] = st + o["dur"]
            fin[i] = st + o["lat"]
            order[o["eng"]].append(i)
            ready.remove(i)
            nleft -= 1
            for k in succ[i]:
                ndeps[k] -= 1
                if ndeps[k] == 0:
                    ready.append(k)
        self.makespan = getattr(self, "makespan", 0.0) + max(fin.values())
        return order

    def finish(self):
        ops = self.ops
        cnt = {n: 0 for n in self.ENGS}
        waited = {n: {} for n in self.ENGS}
        stream = {n: [] for n in self.ENGS}
        tok = {}
        prev_toks = []
        for region in self.regions:
            if not region:
                continue
            order = self._schedule(region)
            for n in self.ENGS:
                for i in order[n]:
                    o = ops[i]
                    if o["dma"] is None:
                        cnt[n] += 1
                        tok[i] = (self.sems[n], cnt[n])
                        o["inc"] = (self.sems[n], 1)
                    else:
                        o["dma"]["cnt"] += 16
                        tok[i] = (o["dma"]["sem"], o["dma"]["cnt"])
                        o["inc"] = (o["dma"]["sem"], 16)
            rset = set(region)
            for n in self.ENGS:
                first = True
                for i in order[n]:
                    o = ops[i]
                    need = {}

                    def add(t):
                        k = id(t[0])
                        if waited[n].get(k, 0) >= t[1]:
                            return
                        if k not in need or need[k][1] < t[1]:
                            need[k] = t
                    if first:
                        for t in prev_toks:
                            add(t)
                        first = False
                    for j in o["deps"]:
                        if j in rset:
                            add(tok[j])
                    for k, t in need.items():
                        waited[n][k] = t[1]
                    stream[n].append((list(need.values()), o["fn"], o["inc"]))
            prev_toks = [(self.sems[n], cnt[n]) for n in self.ENGS if cnt[n] > 0]
            prev_toks += [(d["sem"], d["cnt"]) for d in self.dsems if d["cnt"] > 0]
        final = {n: [] for n in self.ENGS}
        final["sp"] = [t for t in prev_toks if id(t[0]) != id(self.sems["sp"])]
        print("sched: est makespan %.1f us, ops %d" % (getattr(self, "makespan", 0.0), len(ops)))
        nc = self.nc
        with nc.Block() as block:
            def runner(name):
                def body(eng):
                    for waits, fn, inc in stream[name]:
                        for sem, val in waits:
                            eng.wait_ge(sem, val)
                        ins = fn(eng)
                        ins.then_inc(inc[0], inc[1])
                    for sem, val in final[name]:
                        eng.wait_ge(sem, val)
                return body

            block.tensor(runner("pe"))
            block.scalar(runner("act"))
            block.vector(runner("dve"))
            block.gpsimd(runner("pool"))
            block.sync(runner("sp"))


class Arena:
    def __init__(self, ap, size):
        self.ap = ap
        self.size = size
        self.off = 0
        self.peak = 0
        self.log = []

    def alloc(self, nbytes, dtype):
        req = nbytes
        nbytes = (nbytes + 63) // 64 * 64
        o = self.off
        self.off += nbytes
        self.peak = max(self.peak, self.off)
        assert self.off <= self.size, f"arena overflow {self.off} > {self.size}"
        r = self.ap[:, o:o + req].bitcast(dtype)
        self.log.append((o, req, dtype, r))
        return r

    def mark(self):
        return self.off

    def reset(self, m):
        self.off = m


class _Stop(Exception):
    pass


LAYOUT = {}


def build_program(stage="full", dbg=False):
    nc = bass.Bass("TRN2", target_bir_lowering=False)
    dt = nc.dram_tensor
    xseg = dt("xseg", [NSEG, TT, D], F32, kind="ExternalInput").ap()
    w_in = dt("w_in", [D, DIN], F32, kind="ExternalInput").ap()
    w_out = dt("w_out", [2 * D, D], F32, kind="ExternalInput").ap()
    normg_col_d = dt("normg_col", [128, 8], F32, kind="ExternalInput").ap()
    lng_col_d = dt("lng_col", [128, 8], F32, kind="ExternalInput").ap()
    gn_col_d = dt("gn_col", [128, 8], F32, kind="ExternalInput").ap()
    lnb_d = dt("lnb_bc", [128, D], F32, kind="ExternalInput").ap()
    bsb_d = dt("bs_bc", [128, D], F32, kind="ExternalInput").ap()
    lbp_d = dt("lbp", [128, 32], F32, kind="ExternalInput").ap()
    finalg_d = dt("finalg_bc", [128, D], F32, kind="ExternalInput").ap()
    ws_d = dt("ws_t", [128, NH * 128], F32, kind="ExternalInput").ap()
    ident_d = dt("c_ident", [128, 128], F32, kind="ExternalInput").ap()
    mf_d = dt("c_mf", [128, 128], F32, kind="ExternalInput").ap()
    mb_d = dt("c_mb", [128, 128], F32, kind="ExternalInput").ap()
    rmask_d = dt("c_rmask", [128, FL], F32, kind="ExternalInput").ap()
    onesm_d = dt("c_onesm", [128, 128], F32, kind="ExternalInput").ap()
    yseg = dt("yseg", [NSEG, T, D], F32, kind="ExternalOutput").ap()
    w_in_bf = dt("w_in_bf", [D, DIN], BF16, kind="Internal").ap()
    w_out_bf = dt("w_out_bf", [2 * D, D], BF16, kind="Internal").ap()

    stack = ExitStack()
    ARENA_BYTES = 206 * 1024
    arena_t = stack.enter_context(nc.sbuf_tensor("arena", [128, ARENA_BYTES], U8))
    ps_t = stack.enter_context(nc.psum_tensor("ps", [128, 8, 512], F32))
    AR = Arena(arena_t, ARENA_BYTES)
    P = Prog(nc, stack)
    PB = [P.buf("psum", i) for i in range(8)]

    def ps1(i):
        return ps_t[:, i, :]

    def ps2(i):
        return ps_t[:, i:i + 2, :].rearrange("p a b -> p (a b)")

    def ps1_bf(i):
        return ps_t[:, i, :].bitcast(BF16)

    ident_f = AR.alloc(512, F32)
    ident_b = AR.alloc(256, BF16)
    mf = AR.alloc(512, F32)
    mb = AR.alloc(512, F32)
    onesm = AR.alloc(512, F32)
    rmask = AR.alloc(FL * 4, F32)
    normg_col = AR.alloc(32, F32)
    lng_col = AR.alloc(32, F32)
    gn_col = AR.alloc(32, F32)
    lbp = AR.alloc(128, F32)
    lbe = AR.alloc(128, F32)
    lbv = AR.alloc(64, F32)
    lbden = AR.alloc(64, F32)
    sc_col = AR.alloc(64, F32)
    bi_col = AR.alloc(64, F32)
    nsc_col = AR.alloc(64, F32)
    lnsc_col = AR.alloc(64, F32)
    finalg = AR.alloc(4096, F32)
    wsT = AR.alloc(2048, BF16)
    cst = AR.alloc(4096, F32)
    wout_sb = AR.alloc(16 * 1024 * 2, BF16)
    xin = [AR.alloc(4096, F32) for _ in range(2)]
    mixa = AR.alloc(8 * T * 2, BF16)
    vtok = AR.alloc(NBLK * 1024 * 2, BF16)
    gateb = AR.alloc(8 * T * 2, BF16)
    ktT = [AR.alloc(8 * FL * 2, BF16) for _ in range(2)]
    qt = [AR.alloc(8 * T * 2, BF16) for _ in range(2)]
    dend = [AR.alloc(NH * NCH * 4, F32) for _ in range(2)]
    stat = AR.alloc(64 * 4, F32)
    xs6 = AR.alloc(NBLK * 2048, BF16)
    cpow = AR.alloc(8, F32)
    base_mark = AR.mark()

    bs_bc = arena_t[:, base_mark:base_mark + 4096].bitcast(F32)
    hT = AR.alloc(8 * TT * 2, BF16)
    wbuf = [AR.alloc(8 * 512 * 2, BF16) for _ in range(NWB)]
    th = AR.alloc(NTH * 4 * FL * 4, F32)
    blk_mark = AR.mark()
    junk = AR.alloc(2048, BF16)
    sz = [AR.alloc(2048, F32) for _ in range(NSZ)]
    xhat = [AR.alloc(2048, BF16) for _ in range(NXH)]
    tA = AR.alloc(4096, F32)
    blk_end = AR.mark()
    if ALIAS_1B:
        AR.reset(blk_mark)
    gbuf = [AR.alloc(FL * 4, F32) for _ in range(2)]
    kbuf = [AR.alloc(FL * 4, F32) for _ in range(2)]
    bbuf = [AR.alloc(FL * 4, F32) for _ in range(2)]
    ebuf = [AR.alloc(FL * 4, F32) for _ in range(2)]
    AR.reset(max(AR.mark(), blk_end))
    epbuf = [AR.alloc(FL * 4, F32) for _ in range(2)]
    x_peak = AR.mark()
    AR.reset(base_mark)
    shad = [[AR.alloc(2048, BF16) for _ in range(8)] for _ in range(2)]
    y_mark = AR.mark()
    kttok = [AR.alloc(5 * 1024 * 2, BF16) for _ in range(2)]
    smast = [AR.alloc(4096, F32) for _ in range(2)]
    stmp = [AR.alloc(4096, F32) for _ in range(2)]
    y2_peak = AR.mark()
    AR.reset(y_mark)
    t1 = AR.alloc(4096, F32)
    t2 = AR.alloc(4096, F32)
    scT = AR.alloc(2048, BF16)
    sq = AR.alloc(4096, F32)
    rstd_o = AR.alloc(4096, F32)
    t3 = AR.alloc(4096, F32)
    mixb = AR.alloc(2048, BF16)
    rbuf = AR.alloc(4096, F32)
    ybuf = [AR.alloc(4096, F32) for _ in range(2)]
    junk_y = AR.alloc(2048, BF16)
    y3_peak = AR.mark()
    print("arena peaks", x_peak, y2_peak, y3_peak, "of", ARENA_BYTES)

    def _rg(b, ap):
        for (o, req, dty, r) in AR.log:
            if r is ap:
                P.set_range(b, id(ap), o, o + (req + 63) // 64 * 64)
                return
        raise KeyError(b.name)
    _rg(P.buf("hT"), hT)
    for i in range(NWB):
        _rg(P.buf("wbuf", i), wbuf[i])
    for i in range(NTH * 4):
        _rg(P.buf("th", i), th)
    _rg(P.buf("junk"), junk)
    for i in range(NSZ):
        _rg(P.buf("sz", i), sz[i])
    for i in range(NXH):
        _rg(P.buf("xhat", i), xhat[i])
    _rg(P.buf("tA"), tA)
    for i in range(2):
        _rg(P.buf("gbuf", i), gbuf[i])
        _rg(P.buf("kbuf", i), kbuf[i])
        _rg(P.buf("bbuf", i), bbuf[i])
        _rg(P.buf("ebuf", i), ebuf[i])
        _rg(P.buf("epbuf", i), epbuf[i])
    for d_ in range(2):
        for j_ in range(8):
            _rg(P.buf("shad", d_, j_), shad[d_][j_])
        for l_ in range(5):
            _rg(P.buf("kttok", d_, l_), kttok[d_])
        _rg(P.buf("smast", d_), smast[d_])
        _rg(P.buf("ybuf", d_), ybuf[d_])
    for d_ in range(2):
        _rg(P.buf("stmp", d_), stmp[d_])
    for nm_, ap_ in (("t1", t1), ("t2", t2), ("scT", scT), ("sq", sq), ("rstd_o", rstd_o), ("t3", t3),
                     ("mixb", mixb), ("rbuf", rbuf), ("junk_y", junk_y)):
        _rg(P.buf(nm_), ap_)
    if dbg:
        def _flat(v):
            if isinstance(v, (list, tuple)):
                for i, x in enumerate(v):
                    for suf, y in _flat(x):
                        yield ("_%d" % i) + suf, y
            else:
                yield "", v
        for k, v in list(locals().items()):
            for suf, a in _flat(v):
                for (o, req, dty, r) in AR.log:
                    if r is a:
                        LAYOUT[k + suf] = (o, req, "bf16" if dty == BF16 else "f32")

    def v3(ap, a):
        return ap.rearrange("p (a b) -> p a b", a=a)

    def chk(name):
        if stage == name:
            P.stopped = True

    ld = P.dma_sem("ld")
    def dma_in(dst, src, bufs, sem, eng="sp"):
        P.op(eng, lambda e, dst=dst, src=src: e.dma_start(out=dst, in_=src), writes=bufs, dma=sem)

    cb = P.buf("consts")
    for dst, src in [(ident_f, ident_d), (mf, mf_d), (mb, mb_d), (onesm, onesm_d), (rmask, rmask_d),
                     (normg_col, normg_col_d), (lng_col, lng_col_d), (gn_col, gn_col_d), (lbp, lbp_d),
                     (finalg, finalg_d)]:
        dma_in(dst, src, [cb], ld)
    dma_in(tA, ws_d, [P.buf("tA")], P.dma_sem("ld_a"))
    tB = th[:, 0:1024]
    dma_in(tB, lnb_d, [P.buf("tB")], P.dma_sem("ld_b"))
    dma_in(bs_bc, bsb_d, [P.buf("bs_bc")], P.dma_sem("ld_s"))
    chk("s_ld")
    wcb = P.buf("wcast")
    for r in range(4):
        P.op("pool", lambda e, r=r: e.dma_start(out=w_out_bf[r * 512:(r + 1) * 512, :],
                                                in_=w_out[r * 512:(r + 1) * 512, :]),
             writes=[P.buf("wcast_o", r)], dma=P.dma_sem("wco%d" % r))
    for r in range(8):
        P.op("pool", lambda e, r=r: e.dma_start(out=w_in_bf[r * 128:(r + 1) * 128, :],
                                                in_=w_in[r * 128:(r + 1) * 128, :]),
             writes=[P.buf("wcast_i", r)], dma=P.dma_sem("wci%d" % r))
    chk("s_wc")
    cb2 = P.buf("consts2")
    P.op("dve", lambda e: e.tensor_copy(out=ident_b, in_=ident_f), reads=[cb], writes=[cb2])
    P.op("dve", lambda e: e.memset(cpow[:, 0:1], -0.5), writes=[P.buf("cpow")])
    P.op("dve", lambda e: e.memset(cpow[:, 1:2], EPS), writes=[P.buf("cpow")])

    def rsqrt_eps(src, dst, rbufs, wbuf_):
        P.op("pool", lambda e: e.tensor_tensor(out=dst, in0=src, in1=cpow[:, 1:2], op=ALU.add),
             reads=list(rbufs) + [P.buf("cpow")], writes=[wbuf_])
        P.op("pool", lambda e: e.tensor_tensor(out=dst, in0=dst, in1=cpow[:, 0:1], op=ALU.pow),
             reads=[wbuf_, P.buf("cpow")], writes=[wbuf_])
    P.op("act", lambda e: e.activation(out=lbe, in_=lbp, func=AF.Exp), reads=[cb], writes=[P.buf("lbe")])
    lbe4 = lbe.rearrange("p (d l h) -> p d l h", d=2, l=2)
    lb3 = lbv.rearrange("p (d h) -> p d h", d=2)
    lbden3 = lbden.rearrange("p (d h) -> p d h", d=2)
    P.op("dve", lambda e: e.tensor_tensor(out=lbden3, in0=lbe4[:, :, 0, :], in1=lbe4[:, :, 1, :], op=ALU.add),
         reads=[P.buf("lbe")], writes=[P.buf("lbden")])
    P.op("dve", lambda e: e.reciprocal(out=lbden, in_=lbden), reads=[P.buf("lbden")], writes=[P.buf("lbden")])
    P.op("dve", lambda e: e.tensor_tensor(out=lb3, in0=lbe4[:, :, 0, :], in1=lbden3, op=ALU.mult),
         reads=[P.buf("lbe"), P.buf("lbden")], writes=[P.buf("lbv")])
    lbB = P.buf("lbcols")
    P.op("dve", lambda e: e.tensor_scalar(out=sc_col, in0=lbv, scalar1=-0.5, scalar2=0.5, op0=ALU.mult, op1=ALU.add),
         reads=[P.buf("lbv")], writes=[lbB])
    P.op("dve", lambda e: e.tensor_scalar(out=bi_col, in0=lbv, scalar1=0.5, scalar2=0.5, op0=ALU.mult, op1=ALU.add),
         reads=[P.buf("lbv")], writes=[lbB])
    P.op("dve", lambda e: e.tensor_scalar(out=nsc_col, in0=lbv, scalar1=0.5, scalar2=-0.5, op0=ALU.mult, op1=ALU.add),
         reads=[P.buf("lbv")], writes=[lbB])
    P.op("act", lambda e: e.activation(out=lnsc_col, in_=sc_col, func=AF.Ln), reads=[lbB], writes=[lbB])
    chk("s_small")
    wsT3 = v3(wsT, 8)
    wsb = P.buf("wsT")
    tA3 = v3(tA, 8)
    P.op("dve", lambda e: e.tensor_copy(out=wsT, in_=tA), reads=[P.buf("tA")], writes=[wsb])

    def cst_mm(e):
        ins = None
        for h in range(NH):
            ins = e.matmul(ps2(0)[:, h * 128:(h + 1) * 128], lhsT=tB[:, h * 128:(h + 1) * 128], rhs=tA3[:, h, :],
                           start=True, stop=True)
        return ins
    P.op("pe", cst_mm, reads=[P.buf("tA"), P.buf("tB")], writes=[PB[0], PB[1]])
    P.op("dve", lambda e: e.tensor_tensor(out=cst, in0=ps2(0), in1=bs_bc, op=ALU.add),
         reads=[PB[0], PB[1], P.buf("bs_bc")], writes=[P.buf("cst")])
    chk("s_ws")
    wob = P.buf("wout")
    wo3 = v3(wout_sb, 16)
    ld_w = P.dma_sem("ld_w")
    for q4 in range(4):
        P.op("sp", lambda e, q4=q4: e.dma_start(
            out=wo3[:, q4 * 4:(q4 + 1) * 4, :],
            in_=w_out_bf[q4 * 512:(q4 + 1) * 512, :].rearrange("(c p) d -> p c d", p=128)),
            reads=[P.buf("wcast_o", q4)], writes=[P.buf("wout", q4)], dma=P.dma_sem("ld_w%d" % q4))
    P.barrier()

    xld = [P.dma_sem("x0"), P.dma_sem("x1")]
    wld = [P.dma_sem("w%d" % i) for i in range(NWB)]
    yst = [P.dma_sem("y0"), P.dma_sem("y1")]
    w_in_v = w_in_bf.rearrange("(dc p) c -> p dc c", p=128)
    hT3 = v3(hT, 8)
    statB = P.buf("stat")

    rr = dict(ps=0, w=0)

    def load_w(col0):
        slot = rr["w"] % NWB
        rr["w"] += 1
        P.op("sp", lambda e, slot=slot, col0=col0: e.dma_start(out=v3(wbuf[slot], 8), in_=w_in_v[:, :, col0:col0 + 512]),
             reads=[wcb], writes=[P.buf("wbuf", slot)], dma=wld[slot])
        return slot

    def fm_matmul(bank, slot, tcol, tok0, ntok, extra_reads=()):
        def fn(e):
            ins = None
            w3 = v3(wbuf[slot], 8)
            for dc in range(8):
                done = 0
                while done < ntok:
                    n = min(512, ntok - done)
                    ins = e.matmul(ps_t[:, bank + done // 512, 0:n], lhsT=w3[:, dc, tcol * 128:(tcol + 1) * 128],
                                   rhs=hT3[:, dc, tok0 + done: tok0 + done + n], start=(dc == 0), stop=(dc == 7))
                    done += n
            return ins
        wr = [PB[bank]] + ([PB[bank + 1]] if ntok > 512 else [])
        P.op("pe", fn, reads=[P.buf("wbuf", slot), P.buf("hT")] + list(extra_reads), writes=wr)
        return wr

    def tm_matmul(bank, slots, blk):
        def fn(e):
            ins = None
            for half in range(2):
                w3 = v3(wbuf[slots[half]], 8)
                for dc in range(8):
                    ins = e.matmul(ps_t[:, bank + half, :], lhsT=hT3[:, dc, blk * 128:(blk + 1) * 128],
                                   rhs=w3[:, dc, :], start=(dc == 0), stop=(dc == 7))
            return ins
        P.op("pe", fn, reads=[P.buf("wbuf", slots[0]), P.buf("wbuf", slots[1]), P.buf("hT")],
             writes=[PB[bank], PB[bank + 1]])


    xs63 = v3(xs6, NBLK)

    def emit_xnorm(seg):
        for blk in range(NBLK):
            sl = blk % 2
            P.op("sp", lambda e, sl=sl, blk=blk, seg=seg: e.dma_start(out=xin[sl], in_=xseg[seg, blk * 128:(blk + 1) * 128, :]),
                 writes=[P.buf("xin", sl)], dma=xld[sl])
            P.op("act", lambda e, sl=sl: e.activation(out=junk_y, in_=xin[sl], func=AF.Square, scale=1.0 / 32.0,
                                                      accum_out=stat[:, 0:1]),
                 reads=[P.buf("xin", sl)], writes=[P.buf("junk_y"), statB])
            rsqrt_eps(stat[:, 0:1], stat[:, 1:2], [statB], P.buf("stat1"))
            P.op("dve", lambda e, sl=sl, blk=blk: e.tensor_scalar(out=xs63[:, blk, :], in0=xin[sl], scalar1=stat[:, 1:2],
                                                                  scalar2=None, op0=ALU.mult),
                 reads=[P.buf("xin", sl), P.buf("stat1")], writes=[P.buf("xs", blk)])

    try:
      chk("setup")
      emit_xnorm(0)
      P.barrier()
      for seg in range(NSEG):
          P.cur_seg = seg
          for blk in range(NBLK):
              bk = blk % 2

              def tr(e, blk=blk, bk=bk):
                  ins = None
                  pv = v3(ps1_bf(bk), 8)
                  for dc in range(8):
                      ins = e.transpose(out=pv[:, dc, :], in_=xs63[:, blk, dc * 128:(dc + 1) * 128], identity=ident_b)
                  return ins
              P.op("pe", tr, reads=[P.buf("xs", blk), cb2], writes=[PB[bk]])
              P.op("dve", lambda e, bk=bk, blk=blk: e.tensor_tensor(
                  out=hT3[:, :, blk * 128:(blk + 1) * 128], in0=v3(ps1_bf(bk), 8),
                  in1=normg_col.unsqueeze(2).to_broadcast([128, 8, 128]), op=ALU.mult),
                  reads=[PB[bk], cb], writes=[P.buf("hT")])

          chk('p0')
          mixa3 = v3(mixa, 8)
          gateb3 = v3(gateb, 8)
          qs3 = v3(qt[1], 8)
          ktT3 = [v3(ktT[0], 8), v3(ktT[1], 8)]
          qt3 = [v3(qt[0], 8), v3(qt[1], 8)]
          dend3 = [v3(dend[0], 8), v3(dend[1], 8)]
          th3 = v3(th, NTH * 4)
          vtok3 = v3(vtok, NBLK)
          def part_uz():
              for half in range(2):
                  su = load_w(0 + half * 512)
                  szl = load_w(2048 + half * 512)
                  for tcol in range(4):
                      ft = half * 4 + tcol
                      bu = (2 * tcol) % 4
                      bz = bu + 1
                      fm_matmul(bu, su, tcol, HB, T)
                      fm_matmul(bz, szl, tcol, HB, T)
                      s2 = ft % NSZ
                      P.op("act", lambda e, bz=bz, s2=s2: e.activation(out=sz[s2][:, 0:512], in_=ps1(bz), func=AF.Silu),
                           reads=[PB[bz]], writes=[P.buf("sz", s2)])
                      P.op("dve", lambda e, bu=bu, s2=s2, ft=ft: e.tensor_tensor(out=mixa3[:, ft, :], in0=ps1(bu), in1=sz[s2][:, 0:512],
                                                                                op=ALU.mult),
                           reads=[PB[bu], P.buf("sz", s2)], writes=[P.buf("mixa", ft)])

          def part_va():
              s0 = load_w(1024)
              s1 = load_w(1536)
              for b in range(NMB):
                  blk = 1 + b
                  bank = 4 if b % 2 == 0 else 6
                  tm_matmul(bank, (s0, s1), blk)
                  P.op("act", lambda e, bank=bank: e.activation(out=junk, in_=ps2(bank), func=AF.Identity, scale=1.0 / 1024.0,
                                                                accum_out=stat[:, 2:3]),
                       reads=[PB[bank], PB[bank + 1]], writes=[P.buf("junk"), P.buf("stat2")])
                  P.op("act", lambda e, bank=bank: e.activation(out=junk, in_=ps2(bank), func=AF.Square, scale=1.0 / 32.0,
                                                                accum_out=stat[:, 3:4]),
                       reads=[PB[bank], PB[bank + 1]], writes=[P.buf("junk"), P.buf("stat3")])
                  P.op("dve", lambda e: e.tensor_tensor(out=stat[:, 4:5], in0=stat[:, 2:3], in1=stat[:, 2:3], op=ALU.mult),
                       reads=[P.buf("stat2")], writes=[P.buf("stat4")])
                  P.op("dve", lambda e: e.tensor_tensor(out=stat[:, 5:6], in0=stat[:, 3:4], in1=stat[:, 4:5], op=ALU.subtract),
                       reads=[P.buf("stat3"), P.buf("stat4")], writes=[P.buf("stat5")])
                  rsqrt_eps(stat[:, 5:6], stat[:, 6:7], [P.buf("stat5")], P.buf("stat6"))
                  P.op("dve", lambda e: e.scalar_tensor_tensor(out=stat[:, 7:8], in0=stat[:, 2:3], scalar=-1.0, in1=stat[:, 6:7],
                                                               op0=ALU.mult, op1=ALU.mult),
                       reads=[P.buf("stat2"), P.buf("stat6")], writes=[P.buf("stat7")])
                  xsl = b % NXH
                  P.op("act", lambda e, bank=bank, xsl=xsl: e.activation(out=xhat[xsl], in_=ps2(bank), func=AF.Identity,
                                                                         scale=stat[:, 6:7], bias=stat[:, 7:8]),
                       reads=[PB[bank], PB[bank + 1], P.buf("stat6"), P.buf("stat7")], writes=[P.buf("xhat", xsl)])
                  sb = 0 if b % 2 == 0 else 2

                  def spat(e, xsl=xsl, sb=sb):
                      ins = None
                      for h in range(NH):
                          ins = e.matmul(ps2(sb)[:, h * 128:(h + 1) * 128], lhsT=xhat[xsl][:, h * 128:(h + 1) * 128],
                                         rhs=wsT3[:, h, :], start=True, stop=True)
                      return ins
                  P.op("pe", spat, reads=[P.buf("xhat", xsl), wsb], writes=[PB[sb], PB[sb + 1]])
                  P.op("dve", lambda e, sb=sb: e.tensor_tensor(out=v3(tA, 8), in0=v3(ps2(sb), 8),
                                                               in1=lng_col.unsqueeze(2).to_broadcast([128, 8, 128]), op=ALU.mult),
                       reads=[PB[sb], PB[sb + 1], cb], writes=[P.buf("tA")])
                  P.op("dve", lambda e: e.tensor_tensor(out=tA, in0=tA, in1=cst, op=ALU.add),
                       reads=[P.buf("tA"), P.buf("cst")], writes=[P.buf("tA")])
                  P.op("dve", lambda e, b=b: e.tensor_tensor(out=mixa3[:, :, b * 128:(b + 1) * 128], in0=v3(tA, 8),
                                                             in1=mixa3[:, :, b * 128:(b + 1) * 128], op=ALU.mult),
                       reads=[P.buf("tA")] + [P.buf("mixa", ft) for ft in range(8)],
                       writes=[P.buf("mixa", ft) for ft in range(8)])

          def part_q():
              for half in range(2):
                  sq_ = load_w(3072 + half * 512)
                  for tcol in range(4):
                      ft = half * 4 + tcol
                      bk = tcol % 4
                      fm_matmul(bk, sq_, tcol, HB, T)
                      P.op("act", lambda e, bk=bk, ft=ft: e.activation(out=qs3[:, ft, :], in_=ps1(bk), func=AF.Silu),
                           reads=[PB[bk]], writes=[P.buf("qt1", ft)])

          def part_zb():
              for half in range(2):
                  szb = load_w(7168 + half * 512)
                  for tcol in range(4):
                      ft = half * 4 + tcol
                      bk = tcol % 4
                      fm_matmul(bk, szb, tcol, HB, T)
                      s2 = ft % NSZ
                      P.op("act", lambda e, bk=bk, s2=s2: e.activation(out=sz[s2][:, 0:512], in_=ps1(bk), func=AF.Silu),
                           reads=[PB[bk]], writes=[P.buf("sz", s2)])
                      P.op("dve", lambda e, s2=s2, ft=ft: e.tensor_scalar(out=gateb3[:, ft, :], in0=sz[s2][:, 0:512],
                                                                          scalar1=gn_col[:, ft:ft + 1], scalar2=None, op0=ALU.mult),
                           reads=[P.buf("sz", s2), cb], writes=[P.buf("gateb", ft)])

          def part_i():
              s0 = load_w(6144)
              s1 = load_w(6656)
              for blk in range(NBLK):
                  bank = 4 if blk % 2 == 0 else 6
                  tm_matmul(bank, (s0, s1), blk)
                  P.op("act", lambda e, bank=bank, blk=blk: e.activation(out=vtok3[:, blk, :], in_=ps2(bank), func=AF.Copy),
                       reads=[PB[bank], PB[bank + 1]], writes=[P.buf("vtok", blk)])


          def part_f():
              if ALIAS_1B and USE_BARRIERS:
                  P.handoff([P.buf("junk"), P.buf("sz", 0), P.buf("sz", 1), P.buf("xhat", 0), P.buf("xhat", 1), P.buf("tA")],
                            [P.buf(nm, i) for nm in ("gbuf", "kbuf", "bbuf", "ebuf") for i in range(2)])
              for d in range(2):
                  tok0 = 0 if d == 0 else HB
                  moff = HB if d == 0 else 0
                  for half in range(2):
                      sw = load_w(4096 + d * 1024 + half * 512)
                      tb_ = ((d * 2 + half) % NTH) * 4
                      for tcol in range(4):
                          bank = 4 if tcol % 2 == 0 else 6
                          fm_matmul(bank, sw, tcol, tok0, FL)
                          P.op("act", lambda e, bank=bank, ti=tb_ + tcol: e.activation(out=th3[:, ti, :], in_=ps2(bank)[:, 0:FL],
                                                                                  func=AF.Tanh, scale=-0.5),
                               reads=[PB[bank], PB[bank + 1]], writes=[P.buf("th", tb_ + tcol)])
                      for tcol in range(4):
                          ft = half * 4 + tcol
                          s2 = tcol % 2
                          ci = d * 8 + ft
                          P.op("act", lambda e, ti=tb_ + tcol, s2=s2, ci=ci: e.activation(
                              out=gbuf[s2], in_=th3[:, ti, :], func=AF.Ln, scale=nsc_col[:, ci:ci + 1], bias=bi_col[:, ci:ci + 1]),
                              reads=[P.buf("th", tb_ + tcol), lbB], writes=[P.buf("gbuf", s2)])
                          if d == 0:
                              P.op("dve", lambda e, s2=s2: e.tensor_tensor_scan(out=bbuf[s2], data0=rmask, data1=gbuf[s2], initial=0.0,
                                                                                op0=ALU.mult, op1=ALU.add),
                                   reads=[P.buf("gbuf", s2), cb], writes=[P.buf("bbuf", s2)])
                          else:
                              P.op("dve", lambda e, s2=s2: e.tensor_tensor_scan(out=bbuf[s2][:, ::-1], data0=rmask,
                                                                                data1=gbuf[s2][:, ::-1], initial=0.0,
                                                                                op0=ALU.mult, op1=ALU.add),
                                   reads=[P.buf("gbuf", s2), cb], writes=[P.buf("bbuf", s2)])
                          bsrc, bname = bbuf, "bbuf"
                          P.op("act", lambda e, s2=s2, bsrc=bsrc, ci=ci: e.activation(out=ebuf[s2], in_=bsrc[s2], func=AF.Exp, scale=-1.0,
                                                                                     bias=lnsc_col[:, ci:ci + 1]),
                               reads=[P.buf(bname, s2), lbB], writes=[P.buf("ebuf", s2)])
                          P.op("dve", lambda e, s2=s2, d=d, ft=ft, ti=tb_ + tcol: e.scalar_tensor_tensor(
                              out=ktT3[d][:, ft, :], in0=th3[:, ti, :], scalar=1.0, in1=ebuf[s2], op0=ALU.add, op1=ALU.mult),
                               reads=[P.buf("th", tb_ + tcol), P.buf("ebuf", s2)], writes=[P.buf("ktT", d, ft)])
                          P.op("act", lambda e, s2=s2, bsrc=bsrc: e.activation(out=epbuf[s2], in_=bsrc[s2], func=AF.Exp),
                               reads=[P.buf(bname, s2)], writes=[P.buf("epbuf", s2)])
                          P.op("pool", lambda e, s2=s2, d=d, ft=ft, moff=moff: e.tensor_tensor(
                              out=qt3[d][:, ft, :], in0=qs3[:, ft, :], in1=epbuf[s2][:, moff:moff + T], op=ALU.mult),
                              reads=[P.buf("qt1", ft), P.buf("epbuf", s2)],
                              writes=[P.buf("qt%d" % d, ft)])
                          cpos = 63 if d == 0 else 0
                          P.op("dve", lambda e, s2=s2, d=d, ft=ft, cpos=cpos: e.tensor_copy(
                              out=dend3[d][:, ft, :], in_=v3(epbuf[s2], NCH)[:, :, cpos]),
                              reads=[P.buf("epbuf", s2)], writes=[P.buf("dend", d)])


          for _pn in PART_ORDER:
              {'uz': part_uz, 'va': part_va, 'q': part_q, 'zb': part_zb, 'i': part_i, 'f': part_f}[_pn]()
          chk('p1')
          if USE_BARRIERS:
              P.barrier()
          for d in range(2):
              kt3 = v3(kttok[d], 5)
              for lb_ in range(5):
                  bk = lb_ % 2

                  def trk(e, d=d, lb_=lb_, bk=bk):
                      ins = None
                      pv = v3(ps1_bf(bk), 8)
                      for ft in range(8):
                          ins = e.transpose(out=pv[:, ft, :], in_=ktT3[d][:, ft, lb_ * 128:(lb_ + 1) * 128], identity=ident_b)
                      return ins
                  P.op("pe", trk, reads=[P.buf("ktT", d, ft) for ft in range(8)] + [cb2], writes=[PB[bk]])
                  P.op("act", lambda e, bk=bk, lb_=lb_, kt3=kt3: e.activation(out=kt3[:, lb_, :], in_=ps1_bf(bk), func=AF.Copy),
                       reads=[PB[bk]], writes=[P.buf("kttok", d, lb_)])
              gb0 = 0 if d == 0 else 1
              order = list(range(0, NCH - 1)) if d == 0 else list(range(NCH - 1, 0, -1))
              first = True
              for n, c in enumerate(order):
                  bank = 2 + 2 * (n % 3)
                  lb_ = c // 2
                  p0 = (c % 2) * 64

                  def pm(e, d=d, lb_=lb_, p0=p0, bank=bank, kt3=kt3, gb0=gb0):
                      ins = None
                      for h in range(NH):
                          ins = e.matmul(ps2(bank)[:, h * 128:(h + 1) * 128],
                                         lhsT=kt3[p0:p0 + 64, lb_, h * 128:(h + 1) * 128],
                                         rhs=vtok3[p0:p0 + 64, gb0 + lb_, h * 128:(h + 1) * 128], start=True, stop=True)
                      return ins
                  P.op("pe", pm, reads=[P.buf("kttok", d, lb_), P.buf("vtok", gb0 + lb_)], writes=[PB[bank], PB[bank + 1]])
                  dbc = dend3[d][:, :, c:c + 1].to_broadcast([128, 8, 128])
                  if first:
                      P.op("dve", lambda e, bank=bank, d=d, dbc=dbc: e.tensor_tensor(out=v3(smast[d], 8), in0=v3(ps2(bank), 8),
                                                                                    in1=dbc, op=ALU.mult),
                           reads=[PB[bank], PB[bank + 1], P.buf("dend", d)], writes=[P.buf("smast", d)])
                      first = False
                  else:
                      P.op("dve", lambda e, bank=bank, d=d: e.tensor_tensor(out=stmp[d], in0=ps2(bank), in1=smast[d], op=ALU.add),
                           reads=[PB[bank], PB[bank + 1], P.buf("smast", d)], writes=[P.buf("stmp", d)])
                      P.op(MULT_ENG[d], lambda e, d=d, dbc=dbc: e.tensor_tensor(out=v3(smast[d], 8), in0=v3(stmp[d], 8), in1=dbc,
                                                                                 op=ALU.mult),
                           reads=[P.buf("stmp", d), P.buf("dend", d)], writes=[P.buf("smast", d)])
                  tgt = c + 1 if d == 0 else c - 1
                  j = tgt - 2 if d == 0 else tgt
                  if 0 <= j < 8:
                      P.op("act", lambda e, d=d, j=j: e.activation(out=shad[d][j], in_=smast[d], func=AF.Copy),
                           reads=[P.buf("smast", d)], writes=[P.buf("shad", d, j)])
          chk('p2')
          if USE_BARRIERS:
              P.barrier()
          if seg + 1 < NSEG:
              emit_xnorm(seg + 1)
          for b in range(NMB):
              offs = [HB + b * 128, b * 128]
              for d in range(2):
                  bank = 0 if d == 0 else 2

                  def scm(e, d=d, bank=bank, b=b, offs=offs):
                      ins = None
                      for h in range(NH):
                          ins = e.matmul(ps2(bank)[:, h * 128:(h + 1) * 128],
                                         lhsT=ktT3[d][:, h, offs[d]:offs[d] + 128],
                                         rhs=qt3[d][:, h, b * 128:(b + 1) * 128], start=True, stop=True)
                      return ins
                  P.op("pe", scm, reads=[P.buf("ktT", d, ft) for ft in range(8)] + [P.buf("qt%d" % d, ft) for ft in range(8)],
                       writes=[PB[bank], PB[bank + 1]])
              P.op("dve", lambda e: e.tensor_tensor(out=v3(t1, 8), in0=v3(ps2(0), 8),
                                                    in1=mf.unsqueeze(1).to_broadcast([128, 8, 128]), op=ALU.mult),
                   reads=[PB[0], PB[1], cb], writes=[P.buf("t1")])
              P.op("dve", lambda e: e.tensor_tensor(out=v3(t2, 8), in0=v3(ps2(2), 8),
                                                    in1=mb.unsqueeze(1).to_broadcast([128, 8, 128]), op=ALU.mult),
                   reads=[PB[2], PB[3], cb], writes=[P.buf("t2")])
              P.op("dve", lambda e: e.tensor_tensor(out=scT, in0=t1, in1=t2, op=ALU.add),
                   reads=[P.buf("t1"), P.buf("t2")], writes=[P.buf("scT")])
              scT3 = v3(scT, 8)

              def om(e, b=b):
                  ins = None
                  for h in range(NH):
                      hs = slice(h * 128, (h + 1) * 128)
                      for hf in range(2):
                          j = 2 * b + hf
                          cs = slice(h * 128 + hf * 64, h * 128 + hf * 64 + 64)
                          ts = slice(b * 128 + hf * 64, b * 128 + hf * 64 + 64)
                          e.matmul(ps2(4)[:, cs], lhsT=vtok3[:, 1 + b, hs], rhs=scT3[:, h, hf * 64:(hf + 1) * 64],
                                   start=True, stop=False)
                          e.matmul(ps2(4)[:, cs], lhsT=shad[0][j][:, hs], rhs=qt3[0][:, h, ts], start=False, stop=False)
                          ins = e.matmul(ps2(4)[:, cs], lhsT=shad[1][j][:, hs], rhs=qt3[1][:, h, ts], start=False, stop=True)
                  return ins
              P.op("pe", om, reads=[P.buf("vtok", 1 + b), P.buf("scT")] + [P.buf("shad", d, 2 * b + hf) for d in range(2) for hf in range(2)]
                   + [P.buf("qt%d" % d, ft) for d in range(2) for ft in range(8)], writes=[PB[4], PB[5]])
              P.op("act", lambda e: e.activation(out=sq, in_=ps2(4), func=AF.Square), reads=[PB[4], PB[5]], writes=[P.buf("sq")])

              def ssm(e):
                  e.matmul(ps_t[:, 6, :], lhsT=onesm, rhs=sq[:, 0:512], start=True, stop=True)
                  return e.matmul(ps_t[:, 7, :], lhsT=onesm, rhs=sq[:, 512:1024], start=True, stop=True)
              P.op("pe", ssm, reads=[P.buf("sq"), cb], writes=[PB[6], PB[7]])
              P.op("act", lambda e: e.activation(out=rstd_o, in_=ps2(6), func=AF.Ln, bias=cpow[:, 1:2]),
                   reads=[PB[6], PB[7], P.buf("cpow")], writes=[P.buf("rstd_o")])
              P.op("act", lambda e: e.activation(out=rstd_o, in_=rstd_o, func=AF.Exp, scale=-0.5),
                   reads=[P.buf("rstd_o")], writes=[P.buf("rstd_o")])
              P.op("dve", lambda e: e.tensor_tensor(out=t3, in0=ps2(4), in1=rstd_o, op=ALU.mult),
                   reads=[PB[4], PB[5], P.buf("rstd_o")], writes=[P.buf("t3")])
              P.op("dve", lambda e, b=b: e.tensor_tensor(out=v3(mixb, 8), in0=v3(t3, 8), in1=gateb3[:, :, b * 128:(b + 1) * 128],
                                                          op=ALU.mult),
                   reads=[P.buf("t3")] + [P.buf("gateb", ft) for ft in range(8)], writes=[P.buf("mixb")])
              mixb3 = v3(mixb, 8)

              def outp(e, b=b):
                  ins = None
                  for half in range(2):
                      for ec in range(16):
                          lhsT = mixa3[:, ec, b * 128:(b + 1) * 128] if ec < 8 else mixb3[:, ec - 8, :]
                          ins = e.matmul(ps_t[:, 6 + half, :], lhsT=lhsT, rhs=wo3[:, ec, half * 512:(half + 1) * 512],
                                         start=(ec == 0), stop=(ec == 15))
                  return ins
              P.op("pe", outp, reads=[P.buf("mixb"), wob] + [P.buf("mixa", ft) for ft in range(8)], writes=[PB[6], PB[7]])
              sl = b % 2
              P.op("sp", lambda e, sl=sl, b=b, seg=seg: e.dma_start(out=xin[sl], in_=xseg[seg, (1 + b) * 128:(2 + b) * 128, :]),
                   writes=[P.buf("xin", sl)], dma=xld[sl])
              P.op("dve", lambda e, sl=sl: e.tensor_tensor(out=rbuf, in0=ps2(6), in1=xin[sl], op=ALU.add),
                   reads=[PB[6], PB[7], P.buf("xin", sl)], writes=[P.buf("rbuf")])
              P.op("act", lambda e: e.activation(out=junk_y, in_=rbuf, func=AF.Square, scale=1.0 / 32.0, accum_out=stat[:, 8:9]),
                   reads=[P.buf("rbuf")], writes=[P.buf("junk_y"), P.buf("stat8")])
              rsqrt_eps(stat[:, 8:9], stat[:, 9:10], [P.buf("stat8")], P.buf("stat9"))
              P.op("dve", lambda e, sl=sl: e.scalar_tensor_tensor(out=ybuf[sl], in0=rbuf, scalar=stat[:, 9:10], in1=finalg,
                                                                  op0=ALU.mult, op1=ALU.mult),
                   reads=[P.buf("rbuf"), P.buf("stat9"), cb], writes=[P.buf("ybuf", sl)])
              P.op("pool", lambda e, sl=sl, b=b, seg=seg: e.dma_start(out=yseg[seg, b * 128:(b + 1) * 128, :], in_=ybuf[sl]),
                   reads=[P.buf("ybuf", sl)], dma=yst[sl])
          if USE_BARRIERS:
              P.barrier()
          chk('seg1')

    except _Stop:
        pass
    P.stopped = False
    if dbg:
        dbg_arena = dt("dbg_arena", [128, ARENA_BYTES], U8, kind="ExternalOutput").ap()
        dbg_psum = dt("dbg_psum", [128, 4096], F32, kind="ExternalOutput").ap()
        P.barrier()
        dsm = P.dma_sem("dbg")
        CH = ARENA_BYTES // 4
        for i in range(4):
            P.op("sp", lambda e, i=i: e.dma_start(out=dbg_arena[:, i * CH:(i + 1) * CH], in_=arena_t[:, i * CH:(i + 1) * CH]),
                 dma=dsm)
        P.barrier()
        pst = arena_t[:, 0:16384].bitcast(F32)
        P.op("dve", lambda e: e.tensor_copy(out=pst, in_=ps_t[:, :, :].rearrange("p a b -> p (a b)")), writes=[P.buf("pst")])
        P.op("sp", lambda e: e.dma_start(out=dbg_psum, in_=pst), reads=[P.buf("pst")], dma=dsm)
    P.finish()
    stack.close()
    return nc


def _segments():
    segs = []
    for b in range(4):
        for j in range(8192 // T):
            segs.append((0, b, j * T))
    for j in range(16384 // T):
        segs.append((1, 0, j * T))
    return segs


_NC_CACHE = {}


def _in_maps(x_prompt, x_sample, norm_g, w_in, ln_v_g, ln_v_b, w_s, b_s, lb_params, gn_g, w_out, final_g):
    f32 = np.float32
    xs_ = [np.asarray(x_prompt, f32), np.asarray(x_sample, f32)]
    segs = _segments()
    assert len(segs) == NCORES * NSEG
    xseg = np.zeros((NCORES, NSEG, TT, D), f32)
    for gi, (which, b, t0) in enumerate(segs):
        c, s = divmod(gi, NSEG)
        L = xs_[which].shape[1]
        lo, hi = t0 - HB, t0 + T + HB
        slo, shi = max(lo, 0), min(hi, L)
        xseg[c, s, slo - lo: shi - lo, :] = xs_[which][b, slo:shi, :]

    def col8(v):
        return np.ascontiguousarray(np.asarray(v, f32).reshape(8, 128).T)

    consts = {}
    consts["c_ident"] = np.eye(128, dtype=f32)
    si, ti = np.meshgrid(np.arange(128), np.arange(128), indexing="ij")
    same = (si // 64) == (ti // 64)
    consts["c_mf"] = (same & (si <= ti)).astype(f32)
    consts["c_mb"] = (same & (si >= ti)).astype(f32)
    rm = np.ones((128, FL), f32)
    rm[:, 0::64] = 0.0
    consts["c_rmask"] = rm
    consts["c_onesm"] = np.full((128, 128), 1.0 / 128.0, f32)

    shared = dict(
        w_in=np.ascontiguousarray(np.asarray(w_in, f32)[0]),
        w_out=np.ascontiguousarray(np.asarray(w_out, f32)[0]),
        normg_col=col8(norm_g[0]),
        lng_col=col8(ln_v_g[0]),
        gn_col=col8(gn_g[0]),
        lnb_bc=np.ascontiguousarray(np.broadcast_to(np.asarray(ln_v_b, f32).reshape(1, D), (128, D))),
        bs_bc=np.ascontiguousarray(np.broadcast_to(np.asarray(b_s, f32).reshape(1, D), (128, D))),
        lbp=np.ascontiguousarray(np.asarray(lb_params, f32).reshape(2, 2, 8, 128).transpose(3, 0, 1, 2).reshape(128, 32)),
        finalg_bc=np.ascontiguousarray(np.broadcast_to(np.asarray(final_g, f32).reshape(1, D), (128, D))),
        ws_t=np.ascontiguousarray(np.asarray(w_s, f32)[0].transpose(2, 0, 1).reshape(128, NH * 128)),
        **consts,
    )
    in_maps = []
    for c in range(NCORES):
        m = dict(shared)
        m["xseg"] = xseg[c]
        in_maps.append(m)
    return in_maps


def kernel(x_prompt, x_sample, norm_g, w_in, ln_v_g, ln_v_b, w_s, b_s, lb_params, gn_g, w_out, final_g):
    f32 = np.float32
    in_maps = _in_maps(x_prompt, x_sample, norm_g, w_in, ln_v_g, ln_v_b, w_s, b_s, lb_params, gn_g, w_out, final_g)
    segs = _segments()
    if "nc" not in _NC_CACHE:
        _NC_CACHE["nc"] = build_program()
    nc = _NC_CACHE["nc"]
    res = run_bass_kernel_spmd(nc, in_maps, core_ids=list(range(NCORES)))
    y_p = np.zeros((4, 8192, D), f32)
    y_s = np.zeros((1, 16384, D), f32)
    outs = [y_p, y_s]
    for gi, (which, b, t0) in enumerate(segs):
        c, s = divmod(gi, NSEG)
        outs[which][b, t0:t0 + T, :] = res.results[c]["yseg"][s]
    return (y_p, y_s)
```

```python
import numpy as np
import ml_dtypes
from contextlib import ExitStack

import concourse.bass as bass
import concourse.mybir as mybir
from concourse.bass_utils import run_bass_kernel_spmd

F32 = mybir.dt.float32
BF16 = mybir.dt.bfloat16
U8 = mybir.dt.uint8
AF = mybir.ActivationFunctionType
ALU = mybir.AluOpType

NCORES = 8
PART_ORDER = ['q', 'zb', 'uz', 'va', 'f', 'i']
ALIAS_1B = True
NTH = 1
NWB = 3
NSZ = 2
NXH = 2
USE_BARRIERS = False
MULT_ENG = ['dve', 'dve']
D = 1024
NH = 8
T = 512
HB = 128
TT = T + 2 * HB
NBLK = TT // 128
NMB = T // 128
NSEG = 12
FL = T + HB
NCH = FL // 64
DIN = 8192
EPS = 1e-6

DEBUG = False


class Buf:
    __slots__ = ("name", "w", "r", "rng", "partners")

    def __init__(self, name):
        self.name = name
        self.w = None
        self.r = set()
        self.rng = None
        self.partners = None


class _RecIns:
    def then_inc(self, *a, **k):
        return self


class _Rec:
    def __init__(self, eng):
        self.eng = eng
        self.dur = 0.0
        self.tset = None
        self.bytes = 0

    def __getattr__(self, name):
        def call(*a, **k):
            out = k.get("out", a[0] if a else None)
            try:
                n = out.free_size()
            except Exception:
                n = 512
            e = self.eng
            if e == "pe":
                if name == "transpose":
                    self.dur += 0.07
                else:
                    f = 4.0 if k.get("lhsT").dtype == F32 else 1.0
                    self.dur += max(0.035, f * n / 2300.0)
            elif e == "act":
                self.dur += 0.25 + n / 1200.0 + (0.1 if k.get("accum_out") is not None else 0.0)
                fn_ = k.get("func")
                if fn_ in (AF.Silu, AF.Tanh):
                    self.tset = "A"
                elif fn_ in (AF.Ln, AF.Exp):
                    self.tset = "B"
            elif e == "dve":
                if name == "tensor_tensor_scan":
                    self.dur += 0.16 + 2.0 * n / 960.0
                else:
                    self.dur += 0.16 + n / 960.0
            elif e == "pool":
                if name == "dma_start":
                    self.dur += 0.7
                    self.bytes += out.nbytes()
                else:
                    self.dur += 0.2 + n / 480.0
            else:
                self.dur += 0.45
                try:
                    self.bytes += out.nbytes()
                except Exception:
                    pass
            return _RecIns()
        return call


class Prog:
    ENGS = ["pe", "act", "dve", "pool", "sp"]
    SCHED = True
    WIN = 0.25
    TPEN = 0.0

    def __init__(self, nc, stack):
        self.nc = nc
        self.stack = stack
        self.sems = {n: stack.enter_context(nc.semaphore("s_" + n)) for n in self.ENGS}
        self.dsems = []
        self.bufs = {}
        self.stopped = False
        self.ops = []
        self.regions = [[]]

    def buf(self, *key):
        b = self.bufs.get(key)
        if b is None:
            b = Buf(key)
            self.bufs[key] = b
        return b

    def dma_sem(self, name):
        d = dict(sem=self.stack.enter_context(self.nc.semaphore("d_" + name)), cnt=0, name=name)
        self.dsems.append(d)
        return d

    def op(self, eng, fn, reads=(), writes=(), dma=None):
        if self.stopped:
            return None
        oid = len(self.ops)
        deps = set()
        for b in reads:
            if b.w is not None:
                deps.add(b.w)
            for p in self._partners(b):
                if p.w is not None:
                    deps.add(p.w)
        for b in writes:
            if b.w is not None:
                deps.add(b.w)
            deps |= b.r
            for p in self._partners(b):
                if p.w is not None:
                    deps.add(p.w)
                deps |= p.r
        rec = _Rec(eng)
        fn(rec)
        lat = rec.dur
        if dma is not None:
            lat = 2.0 + rec.bytes / 300e3
        self.ops.append(dict(id=oid, eng=eng, fn=fn, deps=deps, dma=dma, dur=rec.dur, lat=lat, tset=rec.tset,
                             region=len(self.regions) - 1))
        self.regions[-1].append(oid)
        for b in reads:
            b.r.add(oid)
        for b in writes:
            b.w = oid
            b.r = set()
        return oid

    def barrier(self):
        if self.regions[-1]:
            self.regions.append([])

    def set_range(self, b, parent, lo, hi):
        b.rng = (parent, lo, hi)
        self.ranged = getattr(self, "ranged", [])
        self.ranged.append(b)
        for x in self.ranged:
            x.partners = None

    def _partners(self, b):
        if b.rng is None:
            return ()
        if b.partners is None:
            pa, lo, hi = b.rng
            b.partners = [x for x in self.ranged if x is not b and x.rng[0] != pa and x.rng[1] < hi and lo < x.rng[2]]
        return b.partners

    def handoff(self, src, dst):
        acc = set()
        for b in src:
            if b.w is not None:
                acc.add(b.w)
            acc |= b.r
        for b in dst:
            b.r |= acc

    def _schedule(self, region):
        ops = self.ops
        rset = set(region)
        order = {n: [] for n in self.ENGS}
        if not self.SCHED:
            for i in region:
                order[ops[i]["eng"]].append(i)
            return order
        succ = {i: [] for i in region}
        ndeps = {}
        for i in region:
            d = [j for j in ops[i]["deps"] if j in rset]
            ops[i]["rdeps"] = d
            ndeps[i] = len(d)
            for j in d:
                succ[j].append(i)
        cp = {}
        for i in reversed(region):
            m = 0.0
            for k in succ[i]:
                if cp[k] > m:
                    m = cp[k]
            cp[i] = ops[i]["lat"] + m
        free = {n: 0.0 for n in self.ENGS}
        fin = {}
        ready = [i for i in region if ndeps[i] == 0]
        cur_set = None
        nleft = len(region)
        WIN = self.WIN
        while nleft:
            cands = []
            mn = None
            for i in ready:
                o = ops[i]
                st = free[o["eng"]]
                for j in o["rdeps"]:
                    t = fin[j] + 0.15
                    if t > st:
                        st = t
                pen = 0.0
                if o["eng"] == "act" and o["tset"] is not None and cur_set is not None and o["tset"] != cur_set:
                    pen = self.TPEN
                cands.append((st + pen, i, st))
                if mn is None or st + pen < mn:
                    mn = st + pen
            best = None
            bkey = None
            for (sp_, i, st) in cands:
                if sp_ <= mn + WIN:
                    key = (-cp[i], sp_, i)
                    if bkey is None or key < bkey:
                        bkey = key
                        best = (i, st)
            i, st = best
            o = ops[i]
            if o["eng"] == "act" and o["tset"] is not None:
                if cur_set is not None and o["tset"] != cur_set:
                    st += 1.3
                cur_set = o["tset"]
            free[o["eng"]] = st + o["dur"]
            fin[i] = st + o["lat"]
            order[o["eng"]].append(i)
            ready.remove(i)
            nleft -= 1
            for k in succ[i]:
                ndeps[k] -= 1
                if ndeps[k] == 0:
                    ready.append(k)
        self.makespan = getattr(self, "makespan", 0.0) + max(fin.values())
        return order

    def finish(self):
        ops = self.ops
        cnt = {n: 0 for n in self.ENGS}
        waited = {n: {} for n in self.ENGS}
        stream = {n: [] for n in self.ENGS}
        tok = {}
        prev_toks = []
        for region in self.regions:
            if not region:
                continue
            order = self._schedule(region)
            for n in self.ENGS:
                for i in order[n]:
                    o = ops[i]
                    if o["dma"] is None:
                        cnt[n] += 1
                        tok[i] = (self.sems[n], cnt[n])
                        o["inc"] = (self.sems[n], 1)
                    else:
                        o["dma"]["cnt"] += 16
                        tok[i] = (o["dma"]["sem"], o["dma"]["cnt"])
                        o["inc"] = (o["dma"]["sem"], 16)
            rset = set(region)
            for n in self.ENGS:
                first = True
                for i in order[n]:
                    o = ops[i]
                    need = {}

                    def add(t):
                        k = id(t[0])
                        if waited[n].get(k, 0) >= t[1]:
                            return
                        if k not in need or need[k][1] < t[1]:
                            need[k] = t
                    if first:
                        for t in prev_toks:
                            add(t)
                        first = False
                    for j in o["deps"]:
                        if j in rset:
                            add(tok[j])
                    for k, t in need.items():
                        waited[n][k] = t[1]
                    stream[n].append((list(need.values()), o["fn"], o["inc"]))
            prev_toks = [(self.sems[n], cnt[n]) for n in self.ENGS if cnt[n] > 0]
            prev_toks += [(d["sem"], d["cnt"]) for d in self.dsems if d["cnt"] > 0]
        final = {n: [] for n in self.ENGS}
        final["sp"] = [t for t in prev_toks if id(t[0]) != id(self.sems["sp"])]
        print("sched: est makespan %.1f us, ops %d" % (getattr(self, "makespan", 0.0), len(ops)))
        nc = self.nc
        with nc.Block() as block:
            def runner(name):
                def body(eng):
                    for waits, fn, inc in stream[name]:
                        for sem, val in waits:
                            eng.wait_ge(sem, val)
                        ins = fn(eng)
                        ins.then_inc(inc[0], inc[1])
                    for sem, val in final[name]:
                        eng.wait_ge(sem, val)
                return body

            block.tensor(runner("pe"))
            block.scalar(runner("act"))
            block.vector(runner("dve"))
            block.gpsimd(runner("pool"))
            block.sync(runner("sp"))


class Arena:
    def __init__(self, ap, size):
        self.ap = ap
        self.size = size
        self.off = 0
        self.peak = 0
        self.log = []

    def alloc(self, nbytes, dtype):
        req = nbytes
        nbytes = (nbytes + 63) // 64 * 64
        o = self.off
        self.off += nbytes
        self.peak = max(self.peak, self.off)
        assert self.off <= self.size, f"arena overflow {self.off} > {self.size}"
        r = self.ap[:, o:o + req].bitcast(dtype)
        self.log.append((o, req, dtype, r))
        return r

    def mark(self):
        return self.off

    def reset(self, m):
        self.off = m


class _Stop(Exception):
    pass


LAYOUT = {}


def build_program(stage="full", dbg=False):
    nc = bass.Bass("TRN2", target_bir_lowering=False)
    dt = nc.dram_tensor
    xseg = dt("xseg", [NSEG, TT, D], F32, kind="ExternalInput").ap()
    w_in = dt("w_in", [D, DIN], F32, kind="ExternalInput").ap()
    w_out = dt("w_out", [2 * D, D], F32, kind="ExternalInput").ap()
    normg_col_d = dt("normg_col", [128, 8], F32, kind="ExternalInput").ap()
    lng_col_d = dt("lng_col", [128, 8], F32, kind="ExternalInput").ap()
    gn_col_d = dt("gn_col", [128, 8], F32, kind="ExternalInput").ap()
    lnb_d = dt("lnb_bc", [128, D], F32, kind="ExternalInput").ap()
    bsb_d = dt("bs_bc", [128, D], F32, kind="ExternalInput").ap()
    lbp_d = dt("lbp", [128, 32], F32, kind="ExternalInput").ap()
    finalg_d = dt("finalg_bc", [128, D], F32, kind="ExternalInput").ap()
    ws_d = dt("ws_t", [128, NH * 128], F32, kind="ExternalInput").ap()
    ident_d = dt("c_ident", [128, 128], F32, kind="ExternalInput").ap()
    mf_d = dt("c_mf", [128, 128], F32, kind="ExternalInput").ap()
    mb_d = dt("c_mb", [128, 128], F32, kind="ExternalInput").ap()
    rmask_d = dt("c_rmask", [128, FL], F32, kind="ExternalInput").ap()
    onesm_d = dt("c_onesm", [128, 128], F32, kind="ExternalInput").ap()
    yseg = dt("yseg", [NSEG, T, D], F32, kind="ExternalOutput").ap()
    w_in_bf = dt("w_in_bf", [D, DIN], BF16, kind="Internal").ap()
    w_out_bf = dt("w_out_bf", [2 * D, D], BF16, kind="Internal").ap()

    stack = ExitStack()
    ARENA_BYTES = 206 * 1024
    arena_t = stack.enter_context(nc.sbuf_tensor("arena", [128, ARENA_BYTES], U8))
    ps_t = stack.enter_context(nc.psum_tensor("ps", [128, 8, 512], F32))
    AR = Arena(arena_t, ARENA_BYTES)
    P = Prog(nc, stack)
    PB = [P.buf("psum", i) for i in range(8)]

    def ps1(i):
        return ps_t[:, i, :]

    def ps2(i):
        return ps_t[:, i:i + 2, :].rearrange("p a b -> p (a b)")

    def ps1_bf(i):
        return ps_t[:, i, :].bitcast(BF16)

    ident_f = AR.alloc(512, F32)
    ident_b = AR.alloc(256, BF16)
    mf = AR.alloc(512, F32)
    mb = AR.alloc(512, F32)
    onesm = AR.alloc(512, F32)
    rmask = AR.alloc(FL * 4, F32)
    normg_col = AR.alloc(32, F32)
    lng_col = AR.alloc(32, F32)
    gn_col = AR.alloc(32, F32)
    lbp = AR.alloc(128, F32)
    lbe = AR.alloc(128, F32)
    lbv = AR.alloc(64, F32)
    lbden = AR.alloc(64, F32)
    sc_col = AR.alloc(64, F32)
    bi_col = AR.alloc(64, F32)
    nsc_col = AR.alloc(64, F32)
    lnsc_col = AR.alloc(64, F32)
    finalg = AR.alloc(4096, F32)
    wsT = AR.alloc(2048, BF16)
    cst = AR.alloc(4096, F32)
    wout_sb = AR.alloc(16 * 1024 * 2, BF16)
    xin = [AR.alloc(4096, F32) for _ in range(2)]
    mixa = AR.alloc(8 * T * 2, BF16)
    vtok = AR.alloc(NBLK * 1024 * 2, BF16)
    gateb = AR.alloc(8 * T * 2, BF16)
    ktT = [AR.alloc(8 * FL * 2, BF16) for _ in range(2)]
    qt = [AR.alloc(8 * T * 2, BF16) for _ in range(2)]
    dend = [AR.alloc(NH * NCH * 4, F32) for _ in range(2)]
    stat = AR.alloc(64 * 4, F32)
    xs6 = AR.alloc(NBLK * 2048, BF16)
    cpow = AR.alloc(8, F32)
    base_mark = AR.mark()

    bs_bc = arena_t[:, base_mark:base_mark + 4096].bitcast(F32)
    hT = AR.alloc(8 * TT * 2, BF16)
    wbuf = [AR.alloc(8 * 512 * 2, BF16) for _ in range(NWB)]
    th = AR.alloc(NTH * 4 * FL * 4, F32)
    blk_mark = AR.mark()
    junk = AR.alloc(2048, BF16)
    sz = [AR.alloc(2048, F32) for _ in range(NSZ)]
    xhat = [AR.alloc(2048, BF16) for _ in range(NXH)]
    tA = AR.alloc(4096, F32)
    blk_end = AR.mark()
    if ALIAS_1B:
        AR.reset(blk_mark)
    gbuf = [AR.alloc(FL * 4, F32) for _ in range(2)]
    kbuf = [AR.alloc(FL * 4, F32) for _ in range(2)]
    bbuf = [AR.alloc(FL * 4, F32) for _ in range(2)]
    ebuf = [AR.alloc(FL * 4, F32) for _ in range(2)]
    AR.reset(max(AR.mark(), blk_end))
    epbuf = [AR.alloc(FL * 4, F32) for _ in range(2)]
    x_peak = AR.mark()
    AR.reset(base_mark)
    shad = [[AR.alloc(2048, BF16) for _ in range(8)] for _ in range(2)]
    y_mark = AR.mark()
    kttok = [AR.alloc(5 * 1024 * 2, BF16) for _ in range(2)]
    smast = [AR.alloc(4096, F32) for _ in range(2)]
    stmp = [AR.alloc(4096, F32) for _ in range(2)]
    y2_peak = AR.mark()
    AR.reset(y_mark)
    t1 = AR.alloc(4096, F32)
    t2 = AR.alloc(4096, F32)
    scT = AR.alloc(2048, BF16)
    sq = AR.alloc(4096, F32)
    rstd_o = AR.alloc(4096, F32)
    t3 = AR.alloc(4096, F32)
    mixb = AR.alloc(2048, BF16)
    rbuf = AR.alloc(4096, F32)
    ybuf = [AR.alloc(4096, F32) for _ in range(2)]
    junk_y = AR.alloc(2048, BF16)
    y3_peak = AR.mark()
    print("arena peaks", x_peak, y2_peak, y3_peak, "of", ARENA_BYTES)

    def _rg(b, ap):
        for (o, req, dty, r) in AR.log:
            if r is ap:
                P.set_range(b, id(ap), o, o + (req + 63) // 64 * 64)
                return
        raise KeyError(b.name)
    _rg(P.buf("hT"), hT)
    for i in range(NWB):
        _rg(P.buf("wbuf", i), wbuf[i])
    for i in range(NTH * 4):
        _rg(P.buf("th", i), th)
    _rg(P.buf("junk"), junk)
    for i in range(NSZ):
        _rg(P.buf("sz", i), sz[i])
    for i in range(NXH):
        _rg(P.buf("xhat", i), xhat[i])
    _rg(P.buf("tA"), tA)
    for i in range(2):
        _rg(P.buf("gbuf", i), gbuf[i])
        _rg(P.buf("kbuf", i), kbuf[i])
        _rg(P.buf("bbuf", i), bbuf[i])
        _rg(P.buf("ebuf", i), ebuf[i])
        _rg(P.buf("epbuf", i), epbuf[i])
    for d_ in range(2):
        for j_ in range(8):
            _rg(P.buf("shad", d_, j_), shad[d_][j_])
        for l_ in range(5):
            _rg(P.buf("kttok", d_, l_), kttok[d_])
        _rg(P.buf("smast", d_), smast[d_])
        _rg(P.buf("ybuf", d_), ybuf[d_])
    for d_ in range(2):
        _rg(P.buf("stmp", d_), stmp[d_])
    for nm_, ap_ in (("t1", t1), ("t2", t2), ("scT", scT), ("sq", sq), ("rstd_o", rstd_o), ("t3", t3),
                     ("mixb", mixb), ("rbuf", rbuf), ("junk_y", junk_y)):
        _rg(P.buf(nm_), ap_)
    if dbg:
        def _flat(v):
            if isinstance(v, (list, tuple)):
                for i, x in enumerate(v):
                    for suf, y in _flat(x):
                        yield ("_%d" % i) + suf, y
            else:
                yield "", v
        for k, v in list(locals().items()):
            for suf, a in _flat(v):
                for (o, req, dty, r) in AR.log:
                    if r is a:
                        LAYOUT[k + suf] = (o, req, "bf16" if dty == BF16 else "f32")

    def v3(ap, a):
        return ap.rearrange("p (a b) -> p a b", a=a)

    def chk(name):
        if stage == name:
            P.stopped = True

    ld = P.dma_sem("ld")
    def dma_in(dst, src, bufs, sem, eng="sp"):
        P.op(eng, lambda e, dst=dst, src=src: e.dma_start(out=dst, in_=src), writes=bufs, dma=sem)

    cb = P.buf("consts")
    for dst, src in [(ident_f, ident_d), (mf, mf_d), (mb, mb_d), (onesm, onesm_d), (rmask, rmask_d),
                     (normg_col, normg_col_d), (lng_col, lng_col_d), (gn_col, gn_col_d), (lbp, lbp_d),
                     (finalg, finalg_d)]:
        dma_in(dst, src, [cb], ld)
    dma_in(tA, ws_d, [P.buf("tA")], P.dma_sem("ld_a"))
    tB = th[:, 0:1024]
    dma_in(tB, lnb_d, [P.buf("tB")], P.dma_sem("ld_b"))
    dma_in(bs_bc, bsb_d, [P.buf("bs_bc")], P.dma_sem("ld_s"))
    chk("s_ld")
    wcb = P.buf("wcast")
    for r in range(4):
        P.op("pool", lambda e, r=r: e.dma_start(out=w_out_bf[r * 512:(r + 1) * 512, :],
                                                in_=w_out[r * 512:(r + 1) * 512, :]),
             writes=[P.buf("wcast_o", r)], dma=P.dma_sem("wco%d" % r))
    for r in range(8):
        P.op("pool", lambda e, r=r: e.dma_start(out=w_in_bf[r * 128:(r + 1) * 128, :],
                                                in_=w_in[r * 128:(r + 1) * 128, :]),
             writes=[P.buf("wcast_i", r)], dma=P.dma_sem("wci%d" % r))
    chk("s_wc")
    cb2 = P.buf("consts2")
    P.op("dve", lambda e: e.tensor_copy(out=ident_b, in_=ident_f), reads=[cb], writes=[cb2])
    P.op("dve", lambda e: e.memset(cpow[:, 0:1], -0.5), writes=[P.buf("cpow")])
    P.op("dve", lambda e: e.memset(cpow[:, 1:2], EPS), writes=[P.buf("cpow")])

    def rsqrt_eps(src, dst, rbufs, wbuf_):
        P.op("pool", lambda e: e.tensor_tensor(out=dst, in0=src, in1=cpow[:, 1:2], op=ALU.add),
             reads=list(rbufs) + [P.buf("cpow")], writes=[wbuf_])
        P.op("pool", lambda e: e.tensor_tensor(out=dst, in0=dst, in1=cpow[:, 0:1], op=ALU.pow),
             reads=[wbuf_, P.buf("cpow")], writes=[wbuf_])
    P.op("act", lambda e: e.activation(out=lbe, in_=lbp, func=AF.Exp), reads=[cb], writes=[P.buf("lbe")])
    lbe4 = lbe.rearrange("p (d l h) -> p d l h", d=2, l=2)
    lb3 = lbv.rearrange("p (d h) -> p d h", d=2)
    lbden3 = lbden.rearrange("p (d h) -> p d h", d=2)
    P.op("dve", lambda e: e.tensor_tensor(out=lbden3, in0=lbe4[:, :, 0, :], in1=lbe4[:, :, 1, :], op=ALU.add),
         reads=[P.buf("lbe")], writes=[P.buf("lbden")])
    P.op("dve", lambda e: e.reciprocal(out=lbden, in_=lbden), reads=[P.buf("lbden")], writes=[P.buf("lbden")])
    P.op("dve", lambda e: e.tensor_tensor(out=lb3, in0=lbe4[:, :, 0, :], in1=lbden3, op=ALU.mult),
         reads=[P.buf("lbe"), P.buf("lbden")], writes=[P.buf("lbv")])
    lbB = P.buf("lbcols")
    P.op("dve", lambda e: e.tensor_scalar(out=sc_col, in0=lbv, scalar1=-0.5, scalar2=0.5, op0=ALU.mult, op1=ALU.add),
         reads=[P.buf("lbv")], writes=[lbB])
    P.op("dve", lambda e: e.tensor_scalar(out=bi_col, in0=lbv, scalar1=0.5, scalar2=0.5, op0=ALU.mult, op1=ALU.add),
         reads=[P.buf("lbv")], writes=[lbB])
    P.op("dve", lambda e: e.tensor_scalar(out=nsc_col, in0=lbv, scalar1=0.5, scalar2=-0.5, op0=ALU.mult, op1=ALU.add),
         reads=[P.buf("lbv")], writes=[lbB])
    P.op("act", lambda e: e.activation(out=lnsc_col, in_=sc_col, func=AF.Ln), reads=[lbB], writes=[lbB])
    chk("s_small")
    wsT3 = v3(wsT, 8)
    wsb = P.buf("wsT")
    tA3 = v3(tA, 8)
    P.op("dve", lambda e: e.tensor_copy(out=wsT, in_=tA), reads=[P.buf("tA")], writes=[wsb])

    def cst_mm(e):
        ins = None
        for h in range(NH):
            ins = e.matmul(ps2(0)[:, h * 128:(h + 1) * 128], lhsT=tB[:, h * 128:(h + 1) * 128], rhs=tA3[:, h, :],
                           start=True, stop=True)
        return ins
    P.op("pe", cst_mm, reads=[P.buf("tA"), P.buf("tB")], writes=[PB[0], PB[1]])
    P.op("dve", lambda e: e.tensor_tensor(out=cst, in0=ps2(0), in1=bs_bc, op=ALU.add),
         reads=[PB[0], PB[1], P.buf("bs_bc")], writes=[P.buf("cst")])
    chk("s_ws")
    wob = P.buf("wout")
    wo3 = v3(wout_sb, 16)
    ld_w = P.dma_sem("ld_w")
    for q4 in range(4):
        P.op("sp", lambda e, q4=q4: e.dma_start(
            out=wo3[:, q4 * 4:(q4 + 1) * 4, :],
            in_=w_out_bf[q4 * 512:(q4 + 1) * 512, :].rearrange("(c p) d -> p c d", p=128)),
            reads=[P.buf("wcast_o", q4)], writes=[P.buf("wout", q4)], dma=P.dma_sem("ld_w%d" % q4))
    P.barrier()

    xld = [P.dma_sem("x0"), P.dma_sem("x1")]
    wld = [P.dma_sem("w%d" % i) for i in range(NWB)]
    yst = [P.dma_sem("y0"), P.dma_sem("y1")]
    w_in_v = w_in_bf.rearrange("(dc p) c -> p dc c", p=128)
    hT3 = v3(hT, 8)
    statB = P.buf("stat")

    rr = dict(ps=0, w=0)

    def load_w(col0):
        slot = rr["w"] % NWB
        rr["w"] += 1
        P.op("sp", lambda e, slot=slot, col0=col0: e.dma_start(out=v3(wbuf[slot], 8), in_=w_in_v[:, :, col0:col0 + 512]),
             reads=[wcb], writes=[P.buf("wbuf", slot)], dma=wld[slot])
        return slot

    def fm_matmul(bank, slot, tcol, tok0, ntok, extra_reads=()):
        def fn(e):
            ins = None
            w3 = v3(wbuf[slot], 8)
            for dc in range(8):
                done = 0
                while done < ntok:
                    n = min(512, ntok - done)
                    ins = e.matmul(ps_t[:, bank + done // 512, 0:n], lhsT=w3[:, dc, tcol * 128:(tcol + 1) * 128],
                                   rhs=hT3[:, dc, tok0 + done: tok0 + done + n], start=(dc == 0), stop=(dc == 7))
                    done += n
            return ins
        wr = [PB[bank]] + ([PB[bank + 1]] if ntok > 512 else [])
        P.op("pe", fn, reads=[P.buf("wbuf", slot), P.buf("hT")] + list(extra_reads), writes=wr)
        return wr

    def tm_matmul(bank, slots, blk):
        def fn(e):
            ins = None
            for half in range(2):
                w3 = v3(wbuf[slots[half]], 8)
                for dc in range(8):
                    ins = e.matmul(ps_t[:, bank + half, :], lhsT=hT3[:, dc, blk * 128:(blk + 1) * 128],
                                   rhs=w3[:, dc, :], start=(dc == 0), stop=(dc == 7))
            return ins
        P.op("pe", fn, reads=[P.buf("wbuf", slots[0]), P.buf("wbuf", slots[1]), P.buf("hT")],
             writes=[PB[bank], PB[bank + 1]])


    xs63 = v3(xs6, NBLK)

    def emit_xnorm(seg):
        for blk in range(NBLK):
            sl = blk % 2
            P.op("sp", lambda e, sl=sl, blk=blk, seg=seg: e.dma_start(out=xin[sl], in_=xseg[seg, blk * 128:(blk + 1) * 128, :]),
                 writes=[P.buf("xin", sl)], dma=xld[sl])
            P.op("act", lambda e, sl=sl: e.activation(out=junk_y, in_=xin[sl], func=AF.Square, scale=1.0 / 32.0,
                                                      accum_out=stat[:, 0:1]),
                 reads=[P.buf("xin", sl)], writes=[P.buf("junk_y"), statB])
            rsqrt_eps(stat[:, 0:1], stat[:, 1:2], [statB], P.buf("stat1"))
            P.op("dve", lambda e, sl=sl, blk=blk: e.tensor_scalar(out=xs63[:, blk, :], in0=xin[sl], scalar1=stat[:, 1:2],
                                                                  scalar2=None, op0=ALU.mult),
                 reads=[P.buf("xin", sl), P.buf("stat1")], writes=[P.buf("xs", blk)])

    try:
      chk("setup")
      emit_xnorm(0)
      P.barrier()
      for seg in range(NSEG):
          P.cur_seg = seg
          for blk in range(NBLK):
              bk = blk % 2

              def tr(e, blk=blk, bk=bk):
                  ins = None
                  pv = v3(ps1_bf(bk), 8)
                  for dc in range(8):
                      ins = e.transpose(out=pv[:, dc, :], in_=xs63[:, blk, dc * 128:(dc + 1) * 128], identity=ident_b)
                  return ins
              P.op("pe", tr, reads=[P.buf("xs", blk), cb2], writes=[PB[bk]])
              P.op("dve", lambda e, bk=bk, blk=blk: e.tensor_tensor(
                  out=hT3[:, :, blk * 128:(blk + 1) * 128], in0=v3(ps1_bf(bk), 8),
                  in1=normg_col.unsqueeze(2).to_broadcast([128, 8, 128]), op=ALU.mult),
                  reads=[PB[bk], cb], writes=[P.buf("hT")])

          chk('p0')
          mixa3 = v3(mixa, 8)
          gateb3 = v3(gateb, 8)
          qs3 = v3(qt[1], 8)
          ktT3 = [v3(ktT[0], 8), v3(ktT[1], 8)]
          qt3 = [v3(qt[0], 8), v3(qt[1], 8)]
          dend3 = [v3(dend[0], 8), v3(dend[1], 8)]
          th3 = v3(th, NTH * 4)
          vtok3 = v3(vtok, NBLK)
          def part_uz():
              for half in range(2):
                  su = load_w(0 + half * 512)
                  szl = load_w(2048 + half * 512)
                  for tcol in range(4):
                      ft = half * 4 + tcol
                      bu = (2 * tcol) % 4
                      bz = bu + 1
                      fm_matmul(bu, su, tcol, HB, T)
                      fm_matmul(bz, szl, tcol, HB, T)
                      s2 = ft % NSZ
                      P.op("act", lambda e, bz=bz, s2=s2: e.activation(out=sz[s2][:, 0:512], in_=ps1(bz), func=AF.Silu),
                           reads=[PB[bz]], writes=[P.buf("sz", s2)])
                      P.op("dve", lambda e, bu=bu, s2=s2, ft=ft: e.tensor_tensor(out=mixa3[:, ft, :], in0=ps1(bu), in1=sz[s2][:, 0:512],
                                                                                op=ALU.mult),
                           reads=[PB[bu], P.buf("sz", s2)], writes=[P.buf("mixa", ft)])

          def part_va():
              s0 = load_w(1024)
              s1 = load_w(1536)
              for b in range(NMB):
                  blk = 1 + b
                  bank = 4 if b % 2 == 0 else 6
                  tm_matmul(bank, (s0, s1), blk)
                  P.op("act", lambda e, bank=bank: e.activation(out=junk, in_=ps2(bank), func=AF.Identity, scale=1.0 / 1024.0,
                                                                accum_out=stat[:, 2:3]),
                       reads=[PB[bank], PB[bank + 1]], writes=[P.buf("junk"), P.buf("stat2")])
                  P.op("act", lambda e, bank=bank: e.activation(out=junk, in_=ps2(bank), func=AF.Square, scale=1.0 / 32.0,
                                                                accum_out=stat[:, 3:4]),
                       reads=[PB[bank], PB[bank + 1]], writes=[P.buf("junk"), P.buf("stat3")])
                  P.op("dve", lambda e: e.tensor_tensor(out=stat[:, 4:5], in0=stat[:, 2:3], in1=stat[:, 2:3], op=ALU.mult),
                       reads=[P.buf("stat2")], writes=[P.buf("stat4")])
                  P.op("dve", lambda e: e.tensor_tensor(out=stat[:, 5:6], in0=stat[:, 3:4], in1=stat[:, 4:5], op=ALU.subtract),
                       reads=[P.buf("stat3"), P.buf("stat4")], writes=[P.buf("stat5")])
                  rsqrt_eps(stat[:, 5:6], stat[:, 6:7], [P.buf("stat5")], P.buf("stat6"))
                  P.op("dve", lambda e: e.scalar_tensor_tensor(out=stat[:, 7:8], in0=stat[:, 2:3], scalar=-1.0, in1=stat[:, 6:7],
                                                               op0=ALU.mult, op1=ALU.mult),
                       reads=[P.buf("stat2"), P.buf("stat6")], writes=[P.buf("stat7")])
                  xsl = b % NXH
                  P.op("act", lambda e, bank=bank, xsl=xsl: e.activation(out=xhat[xsl], in_=ps2(bank), func=AF.Identity,
                                                                         scale=stat[:, 6:7], bias=stat[:, 7:8]),
                       reads=[PB[bank], PB[bank + 1], P.buf("stat6"), P.buf("stat7")], writes=[P.buf("xhat", xsl)])
                  sb = 0 if b % 2 == 0 else 2

                  def spat(e, xsl=xsl, sb=sb):
                      ins = None
                      for h in range(NH):
                          ins = e.matmul(ps2(sb)[:, h * 128:(h + 1) * 128], lhsT=xhat[xsl][:, h * 128:(h + 1) * 128],
                                         rhs=wsT3[:, h, :], start=True, stop=True)
                      return ins
                  P.op("pe", spat, reads=[P.buf("xhat", xsl), wsb], writes=[PB[sb], PB[sb + 1]])
                  P.op("dve", lambda e, sb=sb: e.tensor_tensor(out=v3(tA, 8), in0=v3(ps2(sb), 8),
                                                               in1=lng_col.unsqueeze(2).to_broadcast([128, 8, 128]), op=ALU.mult),
                       reads=[PB[sb], PB[sb + 1], cb], writes=[P.buf("tA")])
                  P.op("dve", lambda e: e.tensor_tensor(out=tA, in0=tA, in1=cst, op=ALU.add),
                       reads=[P.buf("tA"), P.buf("cst")], writes=[P.buf("tA")])
                  P.op("dve", lambda e, b=b: e.tensor_tensor(out=mixa3[:, :, b * 128:(b + 1) * 128], in0=v3(tA, 8),
                                                             in1=mixa3[:, :, b * 128:(b + 1) * 128], op=ALU.mult),
                       reads=[P.buf("tA")] + [P.buf("mixa", ft) for ft in range(8)],
                       writes=[P.buf("mixa", ft) for ft in range(8)])

          def part_q():
              for half in range(2):
                  sq_ = load_w(3072 + half * 512)
                  for tcol in range(4):
                      ft = half * 4 + tcol
                      bk = tcol % 4
                      fm_matmul(bk, sq_, tcol, HB, T)
                      P.op("act", lambda e, bk=bk, ft=ft: e.activation(out=qs3[:, ft, :], in_=ps1(bk), func=AF.Silu),
                           reads=[PB[bk]], writes=[P.buf("qt1", ft)])

          def part_zb():
              for half in range(2):
                  szb = load_w(7168 + half * 512)
                  for tcol in range(4):
                      ft = half * 4 + tcol
                      bk = tcol % 4
                      fm_matmul(bk, szb, tcol, HB, T)
                      s2 = ft % NSZ
                      P.op("act", lambda e, bk=bk, s2=s2: e.activation(out=sz[s2][:, 0:512], in_=ps1(bk), func=AF.Silu),
                           reads=[PB[bk]], writes=[P.buf("sz", s2)])
                      P.op("dve", lambda e, s2=s2, ft=ft: e.tensor_scalar(out=gateb3[:, ft, :], in0=sz[s2][:, 0:512],
                                                                          scalar1=gn_col[:, ft:ft + 1], scalar2=None, op0=ALU.mult),
                           reads=[P.buf("sz", s2), cb], writes=[P.buf("gateb", ft)])

          def part_i():
              s0 = load_w(6144)
              s1 = load_w(6656)
              for blk in range(NBLK):
                  bank = 4 if blk % 2 == 0 else 6
                  tm_matmul(bank, (s0, s1), blk)
                  P.op("act", lambda e, bank=bank, blk=blk: e.activation(out=vtok3[:, blk, :], in_=ps2(bank), func=AF.Copy),
                       reads=[PB[bank], PB[bank + 1]], writes=[P.buf("vtok", blk)])


          def part_f():
              if ALIAS_1B and USE_BARRIERS:
                  P.handoff([P.buf("junk"), P.buf("sz", 0), P.buf("sz", 1), P.buf("xhat", 0), P.buf("xhat", 1), P.buf("tA")],
                            [P.buf(nm, i) for nm in ("gbuf", "kbuf", "bbuf", "ebuf") for i in range(2)])
              for d in range(2):
                  tok0 = 0 if d == 0 else HB
                  moff = HB if d == 0 else 0
                  for half in range(2):
                      sw = load_w(4096 + d * 1024 + half * 512)
                      tb_ = ((d * 2 + half) % NTH) * 4
                      for tcol in range(4):
                          bank = 4 if tcol % 2 == 0 else 6
                          fm_matmul(bank, sw, tcol, tok0, FL)
                          P.op("act", lambda e, bank=bank, ti=tb_ + tcol: e.activation(out=th3[:, ti, :], in_=ps2(bank)[:, 0:FL],
                                                                                  func=AF.Tanh, scale=-0.5),
                               reads=[PB[bank], PB[bank + 1]], writes=[P.buf("th", tb_ + tcol)])
                      for tcol in range(4):
                          ft = half * 4 + tcol
                          s2 = tcol % 2
                          ci = d * 8 + ft
                          P.op("act", lambda e, ti=tb_ + tcol, s2=s2, ci=ci: e.activation(
                              out=gbuf[s2], in_=th3[:, ti, :], func=AF.Ln, scale=nsc_col[:, ci:ci + 1], bias=bi_col[:, ci:ci + 1]),
                              reads=[P.buf("th", tb_ + tcol), lbB], writes=[P.buf("gbuf", s2)])
                          if d == 0:
                              P.op("dve", lambda e, s2=s2: e.tensor_tensor_scan(out=bbuf[s2], data0=rmask, data1=gbuf[s2], initial=0.0,
                                                                                op0=ALU.mult, op1=ALU.add),
                                   reads=[P.buf("gbuf", s2), cb], writes=[P.buf("bbuf", s2)])
                          else:
                              P.op("dve", lambda e, s2=s2: e.tensor_tensor_scan(out=bbuf[s2][:, ::-1], data0=rmask,
                                                                                data1=gbuf[s2][:, ::-1], initial=0.0,
                                                                                op0=ALU.mult, op1=ALU.add),
                                   reads=[P.buf("gbuf", s2), cb], writes=[P.buf("bbuf", s2)])
                          bsrc, bname = bbuf, "bbuf"
                          P.op("act", lambda e, s2=s2, bsrc=bsrc, ci=ci: e.activation(out=ebuf[s2], in_=bsrc[s2], func=AF.Exp, scale=-1.0,
                                                                                     bias=lnsc_col[:, ci:ci + 1]),
                               reads=[P.buf(bname, s2), lbB], writes=[P.buf("ebuf", s2)])
                          P.op("dve", lambda e, s2=s2, d=d, ft=ft, ti=tb_ + tcol: e.scalar_tensor_tensor(
                              out=ktT3[d][:, ft, :], in0=th3[:, ti, :], scalar=1.0, in1=ebuf[s2], op0=ALU.add, op1=ALU.mult),
                               reads=[P.buf("th", tb_ + tcol), P.buf("ebuf", s2)], writes=[P.buf("ktT", d, ft)])
                          P.op("act", lambda e, s2=s2, bsrc=bsrc: e.activation(out=epbuf[s2], in_=bsrc[s2], func=AF.Exp),
                               reads=[P.buf(bname, s2)], writes=[P.buf("epbuf", s2)])
                          P.op("pool", lambda e, s2=s2, d=d, ft=ft, moff=moff: e.tensor_tensor(
                              out=qt3[d][:, ft, :], in0=qs3[:, ft, :], in1=epbuf[s2][:, moff:moff + T], op=ALU.mult),
                              reads=[P.buf("qt1", ft), P.buf("epbuf", s2)],
                              writes=[P.buf("qt%d" % d, ft)])
                          cpos = 63 if d == 0 else 0
                          P.op("dve", lambda e, s2=s2, d=d, ft=ft, cpos=cpos: e.tensor_copy(
                              out=dend3[d][:, ft, :], in_=v3(epbuf[s2], NCH)[:, :, cpos]),
                              reads=[P.buf("epbuf", s2)], writes=[P.buf("dend", d)])


          for _pn in PART_ORDER:
              {'uz': part_uz, 'va': part_va, 'q': part_q, 'zb': part_zb, 'i': part_i, 'f': part_f}[_pn]()
          chk('p1')
          if USE_BARRIERS:
              P.barrier()
          for d in range(2):
              kt3 = v3(kttok[d], 5)
              for lb_ in range(5):
                  bk = lb_ % 2

                  def trk(e, d=d, lb_=lb_, bk=bk):
                      ins = None
                      pv = v3(ps1_bf(bk), 8)
                      for ft in range(8):
                          ins = e.transpose(out=pv[:, ft, :], in_=ktT3[d][:, ft, lb_ * 128:(lb_ + 1) * 128], identity=ident_b)
                      return ins
                  P.op("pe", trk, reads=[P.buf("ktT", d, ft) for ft in range(8)] + [cb2], writes=[PB[bk]])
                  P.op("act", lambda e, bk=bk, lb_=lb_, kt3=kt3: e.activation(out=kt3[:, lb_, :], in_=ps1_bf(bk), func=AF.Copy),
                       reads=[PB[bk]], writes=[P.buf("kttok", d, lb_)])
              gb0 = 0 if d == 0 else 1
              order = list(range(0, NCH - 1)) if d == 0 else list(range(NCH - 1, 0, -1))
              first = True
              for n, c in enumerate(order):
                  bank = 2 + 2 * (n % 3)
                  lb_ = c // 2
                  p0 = (c % 2) * 64

                  def pm(e, d=d, lb_=lb_, p0=p0, bank=bank, kt3=kt3, gb0=gb0):
                      ins = None
                      for h in range(NH):
                          ins = e.matmul(ps2(bank)[:, h * 128:(h + 1) * 128],
                                         lhsT=kt3[p0:p0 + 64, lb_, h * 128:(h + 1) * 128],
                                         rhs=vtok3[p0:p0 + 64, gb0 + lb_, h * 128:(h + 1) * 128], start=True, stop=True)
                      return ins
                  P.op("pe", pm, reads=[P.buf("kttok", d, lb_), P.buf("vtok", gb0 + lb_)], writes=[PB[bank], PB[bank + 1]])
                  dbc = dend3[d][:, :, c:c + 1].to_broadcast([128, 8, 128])
                  if first:
                      P.op("dve", lambda e, bank=bank, d=d, dbc=dbc: e.tensor_tensor(out=v3(smast[d], 8), in0=v3(ps2(bank), 8),
                                                                                    in1=dbc, op=ALU.mult),
                           reads=[PB[bank], PB[bank + 1], P.buf("dend", d)], writes=[P.buf("smast", d)])
                      first = False
                  else:
                      P.op("dve", lambda e, bank=bank, d=d: e.tensor_tensor(out=stmp[d], in0=ps2(bank), in1=smast[d], op=ALU.add),
                           reads=[PB[bank], PB[bank + 1], P.buf("smast", d)], writes=[P.buf("stmp", d)])
                      P.op(MULT_ENG[d], lambda e, d=d, dbc=dbc: e.tensor_tensor(out=v3(smast[d], 8), in0=v3(stmp[d], 8), in1=dbc,
                                                                                 op=ALU.mult),
                           reads=[P.buf("stmp", d), P.buf("dend", d)], writes=[P.buf("smast", d)])
                  tgt = c + 1 if d == 0 else c - 1
                  j = tgt - 2 if d == 0 else tgt
                  if 0 <= j < 8:
                      P.op("act", lambda e, d=d, j=j: e.activation(out=shad[d][j], in_=smast[d], func=AF.Copy),
                           reads=[P.buf("smast", d)], writes=[P.buf("shad", d, j)])
          chk('p2')
          if USE_BARRIERS:
              P.barrier()
          if seg + 1 < NSEG:
              emit_xnorm(seg + 1)
          for b in range(NMB):
              offs = [HB + b * 128, b * 128]
              for d in range(2):
                  bank = 0 if d == 0 else 2

                  def scm(e, d=d, bank=bank, b=b, offs=offs):
                      ins = None
                      for h in range(NH):
                          ins = e.matmul(ps2(bank)[:, h * 128:(h + 1) * 128],
                                         lhsT=ktT3[d][:, h, offs[d]:offs[d] + 128],
                                         rhs=qt3[d][:, h, b * 128:(b + 1) * 128], start=True, stop=True)
                      return ins
                  P.op("pe", scm, reads=[P.buf("ktT", d, ft) for ft in range(8)] + [P.buf("qt%d" % d, ft) for ft in range(8)],
                       writes=[PB[bank], PB[bank + 1]])
              P.op("dve", lambda e: e.tensor_tensor(out=v3(t1, 8), in0=v3(ps2(0), 8),
                                                    in1=mf.unsqueeze(1).to_broadcast([128, 8, 128]), op=ALU.mult),
                   reads=[PB[0], PB[1], cb], writes=[P.buf("t1")])
              P.op("dve", lambda e: e.tensor_tensor(out=v3(t2, 8), in0=v3(ps2(2), 8),
                                                    in1=mb.unsqueeze(1).to_broadcast([128, 8, 128]), op=ALU.mult),
                   reads=[PB[2], PB[3], cb], writes=[P.buf("t2")])
              P.op("dve", lambda e: e.tensor_tensor(out=scT, in0=t1, in1=t2, op=ALU.add),
                   reads=[P.buf("t1"), P.buf("t2")], writes=[P.buf("scT")])
              scT3 = v3(scT, 8)

              def om(e, b=b):
                  ins = None
                  for h in range(NH):
                      hs = slice(h * 128, (h + 1) * 128)
                      for hf in range(2):
                          j = 2 * b + hf
                          cs = slice(h * 128 + hf * 64, h * 128 + hf * 64 + 64)
                          ts = slice(b * 128 + hf * 64, b * 128 + hf * 64 + 64)
                          e.matmul(ps2(4)[:, cs], lhsT=vtok3[:, 1 + b, hs], rhs=scT3[:, h, hf * 64:(hf + 1) * 64],
                                   start=True, stop=False)
                          e.matmul(ps2(4)[:, cs], lhsT=shad[0][j][:, hs], rhs=qt3[0][:, h, ts], start=False, stop=False)
                          ins = e.matmul(ps2(4)[:, cs], lhsT=shad[1][j][:, hs], rhs=qt3[1][:, h, ts], start=False, stop=True)
                  return ins
              P.op("pe", om, reads=[P.buf("vtok", 1 + b), P.buf("scT")] + [P.buf("shad", d, 2 * b + hf) for d in range(2) for hf in range(2)]
                   + [P.buf("qt%d" % d, ft) for d in range(2) for ft in range(8)], writes=[PB[4], PB[5]])
              P.op("act", lambda e: e.activation(out=sq, in_=ps2(4), func=AF.Square), reads=[PB[4], PB[5]], writes=[P.buf("sq")])

              def ssm(e):
                  e.matmul(ps_t[:, 6, :], lhsT=onesm, rhs=sq[:, 0:512], start=True, stop=True)
                  return e.matmul(ps_t[:, 7, :], lhsT=onesm, rhs=sq[:, 512:1024], start=True, stop=True)
              P.op("pe", ssm, reads=[P.buf("sq"), cb], writes=[PB[6], PB[7]])
              P.op("act", lambda e: e.activation(out=rstd_o, in_=ps2(6), func=AF.Ln, bias=cpow[:, 1:2]),
                   reads=[PB[6], PB[7], P.buf("cpow")], writes=[P.buf("rstd_o")])
              P.op("act", lambda e: e.activation(out=rstd_o, in_=rstd_o, func=AF.Exp, scale=-0.5),
                   reads=[P.buf("rstd_o")], writes=[P.buf("rstd_o")])
              P.op("dve", lambda e: e.tensor_tensor(out=t3, in0=ps2(4), in1=rstd_o, op=ALU.mult),
                   reads=[PB[4], PB[5], P.buf("rstd_o")], writes=[P.buf("t3")])
              P.op("dve", lambda e, b=b: e.tensor_tensor(out=v3(mixb, 8), in0=v3(t3, 8), in1=gateb3[:, :, b * 128:(b + 1) * 128],
                                                          op=ALU.mult),
                   reads=[P.buf("t3")] + [P.buf("gateb", ft) for ft in range(8)], writes=[P.buf("mixb")])
              mixb3 = v3(mixb, 8)

              def outp(e, b=b):
                  ins = None
                  for half in range(2):
                      for ec in range(16):
                          lhsT = mixa3[:, ec, b * 128:(b + 1) * 128] if ec < 8 else mixb3[:, ec - 8, :]
                          ins = e.matmul(ps_t[:, 6 + half, :], lhsT=lhsT, rhs=wo3[:, ec, half * 512:(half + 1) * 512],
                                         start=(ec == 0), stop=(ec == 15))
                  return ins
              P.op("pe", outp, reads=[P.buf("mixb"), wob] + [P.buf("mixa", ft) for ft in range(8)], writes=[PB[6], PB[7]])
              sl = b % 2
              P.op("sp", lambda e, sl=sl, b=b, seg=seg: e.dma_start(out=xin[sl], in_=xseg[seg, (1 + b) * 128:(2 + b) * 128, :]),
                   writes=[P.buf("xin", sl)], dma=xld[sl])
              P.op("dve", lambda e, sl=sl: e.tensor_tensor(out=rbuf, in0=ps2(6), in1=xin[sl], op=ALU.add),
                   reads=[PB[6], PB[7], P.buf("xin", sl)], writes=[P.buf("rbuf")])
              P.op("act", lambda e: e.activation(out=junk_y, in_=rbuf, func=AF.Square, scale=1.0 / 32.0, accum_out=stat[:, 8:9]),
                   reads=[P.buf("rbuf")], writes=[P.buf("junk_y"), P.buf("stat8")])
              rsqrt_eps(stat[:, 8:9], stat[:, 9:10], [P.buf("stat8")], P.buf("stat9"))
              P.op("dve", lambda e, sl=sl: e.scalar_tensor_tensor(out=ybuf[sl], in0=rbuf, scalar=stat[:, 9:10], in1=finalg,
                                                                  op0=ALU.mult, op1=ALU.mult),
                   reads=[P.buf("rbuf"), P.buf("stat9"), cb], writes=[P.buf("ybuf", sl)])
              P.op("pool", lambda e, sl=sl, b=b, seg=seg: e.dma_start(out=yseg[seg, b * 128:(b + 1) * 128, :], in_=ybuf[sl]),
                   reads=[P.buf("ybuf", sl)], dma=yst[sl])
          if USE_BARRIERS:
              P.barrier()
          chk('seg1')

    except _Stop:
        pass
    P.stopped = False
    if dbg:
        dbg_arena = dt("dbg_arena", [128, ARENA_BYTES], U8, kind="ExternalOutput").ap()
        dbg_psum = dt("dbg_psum", [128, 4096], F32, kind="ExternalOutput").ap()
        P.barrier()
        dsm = P.dma_sem("dbg")
        CH = ARENA_BYTES // 4
        for i in range(4):
            P.op("sp", lambda e, i=i: e.dma_start(out=dbg_arena[:, i * CH:(i + 1) * CH], in_=arena_t[:, i * CH:(i + 1) * CH]),
                 dma=dsm)
        P.barrier()
        pst = arena_t[:, 0:16384].bitcast(F32)
        P.op("dve", lambda e: e.tensor_copy(out=pst, in_=ps_t[:, :, :].rearrange("p a b -> p (a b)")), writes=[P.buf("pst")])
        P.op("sp", lambda e: e.dma_start(out=dbg_psum, in_=pst), reads=[P.buf("pst")], dma=dsm)
    P.finish()
    stack.close()
    return nc


def _segments():
    segs = []
    for b in range(4):
        for j in range(8192 // T):
            segs.append((0, b, j * T))
    for j in range(16384 // T):
        segs.append((1, 0, j * T))
    return segs


_NC_CACHE = {}


def _in_maps(x_prompt, x_sample, norm_g, w_in, ln_v_g, ln_v_b, w_s, b_s, lb_params, gn_g, w_out, final_g):
    f32 = np.float32
    xs_ = [np.asarray(x_prompt, f32), np.asarray(x_sample, f32)]
    segs = _segments()
    assert len(segs) == NCORES * NSEG
    xseg = np.zeros((NCORES, NSEG, TT, D), f32)
    for gi, (which, b, t0) in enumerate(segs):
        c, s = divmod(gi, NSEG)
        L = xs_[which].shape[1]
        lo, hi = t0 - HB, t0 + T + HB
        slo, shi = max(lo, 0), min(hi, L)
        xseg[c, s, slo - lo: shi - lo, :] = xs_[which][b, slo:shi, :]

    def col8(v):
        return np.ascontiguousarray(np.asarray(v, f32).reshape(8, 128).T)

    consts = {}
    consts["c_ident"] = np.eye(128, dtype=f32)
    si, ti = np.meshgrid(np.arange(128), np.arange(128), indexing="ij")
    same = (si // 64) == (ti // 64)
    consts["c_mf"] = (same & (si <= ti)).astype(f32)
    consts["c_mb"] = (same & (si >= ti)).astype(f32)
    rm = np.ones((128, FL), f32)
    rm[:, 0::64] = 0.0
    consts["c_rmask"] = rm
    consts["c_onesm"] = np.full((128, 128), 1.0 / 128.0, f32)

    shared = dict(
        w_in=np.ascontiguousarray(np.asarray(w_in, f32)[0]),
        w_out=np.ascontiguousarray(np.asarray(w_out, f32)[0]),
        normg_col=col8(norm_g[0]),
        lng_col=col8(ln_v_g[0]),
        gn_col=col8(gn_g[0]),
        lnb_bc=np.ascontiguousarray(np.broadcast_to(np.asarray(ln_v_b, f32).reshape(1, D), (128, D))),
        bs_bc=np.ascontiguousarray(np.broadcast_to(np.asarray(b_s, f32).reshape(1, D), (128, D))),
        lbp=np.ascontiguousarray(np.asarray(lb_params, f32).reshape(2, 2, 8, 128).transpose(3, 0, 1, 2).reshape(128, 32)),
        finalg_bc=np.ascontiguousarray(np.broadcast_to(np.asarray(final_g, f32).reshape(1, D), (128, D))),
        ws_t=np.ascontiguousarray(np.asarray(w_s, f32)[0].transpose(2, 0, 1).reshape(128, NH * 128)),
        **consts,
    )
    in_maps = []
    for c in range(NCORES):
        m = dict(shared)
        m["xseg"] = xseg[c]
        in_maps.append(m)
    return in_maps


def kernel(x_prompt, x_sample, norm_g, w_in, ln_v_g, ln_v_b, w_s, b_s, lb_params, gn_g, w_out, final_g):
    f32 = np.float32
    in_maps = _in_maps(x_prompt, x_sample, norm_g, w_in, ln_v_g, ln_v_b, w_s, b_s, lb_params, gn_g, w_out, final_g)
    segs = _segments()
    if "nc" not in _NC_CACHE:
        _NC_CACHE["nc"] = build_program()
    nc = _NC_CACHE["nc"]
    res = run_bass_kernel_spmd(nc, in_maps, core_ids=list(range(NCORES)))
    y_p = np.zeros((4, 8192, D), f32)
    y_s = np.zeros((1, 16384, D), f32)
    outs = [y_p, y_s]
    for gi, (which, b, t0) in enumerate(segs):
        c, s = divmod(gi, NSEG)
        outs[which][b, t0:t0 + T, :] = res.results[c]["yseg"][s]
    return (y_p, y_s)
```

```python
import numpy as np
import ml_dtypes
from contextlib import ExitStack

import concourse.bass as bass
import concourse.mybir as mybir
from concourse.bass_utils import run_bass_kernel_spmd

F32 = mybir.dt.float32
BF16 = mybir.dt.bfloat16
U8 = mybir.dt.uint8
AF = mybir.ActivationFunctionType
ALU = mybir.AluOpType

NCORES = 8
PART_ORDER = ['zb', 'uz', 'va', 'q', 'f', 'i']
ALIAS_1B = True
NTH = 1
NWB = 3
NSZ = 2
NXH = 2
USE_BARRIERS = False
MULT_ENG = ['dve', 'dve']
D = 1024
NH = 8
T = 512
HB = 128
TT = T + 2 * HB
NBLK = TT // 128
NMB = T // 128
NSEG = 12
FL = T + HB
NCH = FL // 64
DIN = 8192
EPS = 1e-6

DEBUG = False


class Buf:
    __slots__ = ("name", "w", "r", "rng", "partners")

    def __init__(self, name):
        self.name = name
        self.w = None
        self.r = set()
        self.rng = None
        self.partners = None


class _RecIns:
    def then_inc(self, *a, **k):
        return self


class _Rec:
    def __init__(self, eng):
        self.eng = eng
        self.dur = 0.0
        self.tset = None
        self.bytes = 0

    def __getattr__(self, name):
        def call(*a, **k):
            out = k.get("out", a[0] if a else None)
            try:
                n = out.free_size()
            except Exception:
                n = 512
            e = self.eng
            if e == "pe":
                if name == "transpose":
                    self.dur += 0.07
                else:
                    f = 4.0 if k.get("lhsT").dtype == F32 else 1.0
                    self.dur += max(0.035, f * n / 2300.0)
            elif e == "act":
                self.dur += 0.25 + n / 1200.0 + (0.1 if k.get("accum_out") is not None else 0.0)
                fn_ = k.get("func")
                if fn_ in (AF.Silu, AF.Tanh):
                    self.tset = "A"
                elif fn_ in (AF.Ln, AF.Exp):
                    self.tset = "B"
            elif e == "dve":
                if name == "tensor_tensor_scan":
                    self.dur += 0.16 + 2.0 * n / 960.0
                else:
                    self.dur += 0.16 + n / 960.0
            elif e == "pool":
                if name == "dma_start":
                    self.dur += 0.7
                    self.bytes += out.nbytes()
                else:
                    self.dur += 0.2 + n / 480.0
            else:
                self.dur += 0.45
                try:
                    self.bytes += out.nbytes()
                except Exception:
                    pass
            return _RecIns()
        return call


class Prog:
    ENGS = ["pe", "act", "dve", "pool", "sp"]
    SCHED = True
    WIN = 0.5
    TPEN = 0.0

    def __init__(self, nc, stack):
        self.nc = nc
        self.stack = stack
        self.sems = {n: stack.enter_context(nc.semaphore("s_" + n)) for n in self.ENGS}
        self.dsems = []
        self.bufs = {}
        self.stopped = False
        self.ops = []
        self.regions = [[]]

    def buf(self, *key):
        b = self.bufs.get(key)
        if b is None:
            b = Buf(key)
            self.bufs[key] = b
        return b

    def dma_sem(self, name):
        d = dict(sem=self.stack.enter_context(self.nc.semaphore("d_" + name)), cnt=0, name=name)
        self.dsems.append(d)
        return d

    def op(self, eng, fn, reads=(), writes=(), dma=None):
        if self.stopped:
            return None
        oid = len(self.ops)
        deps = set()
        for b in reads:
            if b.w is not None:
                deps.add(b.w)
            for p in self._partners(b):
                if p.w is not None:
                    deps.add(p.w)
        for b in writes:
            if b.w is not None:
                deps.add(b.w)
            deps |= b.r
            for p in self._partners(b):
                if p.w is not None:
                    deps.add(p.w)
                deps |= p.r
        rec = _Rec(eng)
        fn(rec)
        lat = rec.dur
        if dma is not None:
            lat = 2.0 + rec.bytes / 300e3
        self.ops.append(dict(id=oid, eng=eng, fn=fn, deps=deps, dma=dma, dur=rec.dur, lat=lat, tset=rec.tset,
                             region=len(self.regions) - 1))
        self.regions[-1].append(oid)
        for b in reads:
            b.r.add(oid)
        for b in writes:
            b.w = oid
            b.r = set()
        return oid

    def barrier(self):
        if self.regions[-1]:
            self.regions.append([])

    def set_range(self, b, parent, lo, hi):
        b.rng = (parent, lo, hi)
        self.ranged = getattr(self, "ranged", [])
        self.ranged.append(b)
        for x in self.ranged:
            x.partners = None

    def _partners(self, b):
        if b.rng is None:
            return ()
        if b.partners is None:
            pa, lo, hi = b.rng
            b.partners = [x for x in self.ranged if x is not b and x.rng[0] != pa and x.rng[1] < hi and lo < x.rng[2]]
        return b.partners

    def handoff(self, src, dst):
        acc = set()
        for b in src:
            if b.w is not None:
                acc.add(b.w)
            acc |= b.r
        for b in dst:
            b.r |= acc

    def _schedule(self, region):
        ops = self.ops
        rset = set(region)
        order = {n: [] for n in self.ENGS}
        if not self.SCHED:
            for i in region:
                order[ops[i]["eng"]].append(i)
            return order
        succ = {i: [] for i in region}
        ndeps = {}
        for i in region:
            d = [j for j in ops[i]["deps"] if j in rset]
            ops[i]["rdeps"] = d
            ndeps[i] = len(d)
            for j in d:
                succ[j].append(i)
        cp = {}
        for i in reversed(region):
            m = 0.0
            for k in succ[i]:
                if cp[k] > m:
                    m = cp[k]
            cp[i] = ops[i]["lat"] + m
        free = {n: 0.0 for n in self.ENGS}
        fin = {}
        ready = [i for i in region if ndeps[i] == 0]
        cur_set = None
        nleft = len(region)
        WIN = self.WIN
        while nleft:
            cands = []
            mn = None
            for i in ready:
                o = ops[i]
                st = free[o["eng"]]
                for j in o["rdeps"]:
                    t = fin[j] + 0.15
                    if t > st:
                        st = t
                pen = 0.0
                if o["eng"] == "act" and o["tset"] is not None and cur_set is not None and o["tset"] != cur_set:
                    pen = self.TPEN
                cands.append((st + pen, i, st))
                if mn is None or st + pen < mn:
                    mn = st + pen
            best = None
            bkey = None
            for (sp_, i, st) in cands:
                if sp_ <= mn + WIN:
                    key = (-cp[i], sp_, i)
                    if bkey is None or key < bkey:
                        bkey = key
                        best = (i, st)
            i, st = best
            o = ops[i]
            if o["eng"] == "act" and o["tset"] is not None:
                if cur_set is not None and o["tset"] != cur_set:
                    st += 1.3
                cur_set = o["tset"]
            free[o["eng"]] = st + o["dur"]
            fin[i] = st + o["lat"]
            order[o["eng"]].append(i)
            ready.remove(i)
            nleft -= 1
            for k in succ[i]:
                ndeps[k] -= 1
                if ndeps[k] == 0:
                    ready.append(k)
        self.makespan = getattr(self, "makespan", 0.0) + max(fin.values())
        return order

    def finish(self):
        ops = self.ops
        cnt = {n: 0 for n in self.ENGS}
        waited = {n: {} for n in self.ENGS}
        stream = {n: [] for n in self.ENGS}
        tok = {}
        prev_toks = []
        for region in self.regions:
            if not region:
                continue
            order = self._schedule(region)
            for n in self.ENGS:
                for i in order[n]:
                    o = ops[i]
                    if o["dma"] is None:
                        cnt[n] += 1
                        tok[i] = (self.sems[n], cnt[n])
                        o["inc"] = (self.sems[n], 1)
                    else:
                        o["dma"]["cnt"] += 16
                        tok[i] = (o["dma"]["sem"], o["dma"]["cnt"])
                        o["inc"] = (o["dma"]["sem"], 16)
            rset = set(region)
            for n in self.ENGS:
                first = True
                for i in order[n]:
                    o = ops[i]
                    need = {}

                    def add(t):
                        k = id(t[0])
                        if waited[n].get(k, 0) >= t[1]:
                            return
                        if k not in need or need[k][1] < t[1]:
                            need[k] = t
                    if first:
                        for t in prev_toks:
                            add(t)
                        first = False
                    for j in o["deps"]:
                        if j in rset:
                            add(tok[j])
                    for k, t in need.items():
                        waited[n][k] = t[1]
                    stream[n].append((list(need.values()), o["fn"], o["inc"]))
            prev_toks = [(self.sems[n], cnt[n]) for n in self.ENGS if cnt[n] > 0]
            prev_toks += [(d["sem"], d["cnt"]) for d in self.dsems if d["cnt"] > 0]
        final = {n: [] for n in self.ENGS}
        final["sp"] = [t for t in prev_toks if id(t[0]) != id(self.sems["sp"])]
        print("sched: est makespan %.1f us, ops %d" % (getattr(self, "makespan", 0.0), len(ops)))
        nc = self.nc
        with nc.Block() as block:
            def runner(name):
                def body(eng):
                    for waits, fn, inc in stream[name]:
                        for sem, val in waits:
                            eng.wait_ge(sem, val)
                        ins = fn(eng)
                        ins.then_inc(inc[0], inc[1])
                    for sem, val in final[name]:
                        eng.wait_ge(sem, val)
                return body

            block.tensor(runner("pe"))
            block.scalar(runner("act"))
            block.vector(runner("dve"))
            block.gpsimd(runner("pool"))
            block.sync(runner("sp"))


class Arena:
    def __init__(self, ap, size):
        self.ap = ap
        self.size = size
        self.off = 0
        self.peak = 0
        self.log = []

    def alloc(self, nbytes, dtype):
        req = nbytes
        nbytes = (nbytes + 63) // 64 * 64
        o = self.off
        self.off += nbytes
        self.peak = max(self.peak, self.off)
        assert self.off <= self.size, f"arena overflow {self.off} > {self.size}"
        r = self.ap[:, o:o + req].bitcast(dtype)
        self.log.append((o, req, dtype, r))
        return r

    def mark(self):
        return self.off

    def reset(self, m):
        self.off = m


class _Stop(Exception):
    pass


LAYOUT = {}


def build_program(stage="full", dbg=False):
    nc = bass.Bass("TRN2", target_bir_lowering=False)
    dt = nc.dram_tensor
    xseg = dt("xseg", [NSEG, TT, D], F32, kind="ExternalInput").ap()
    w_in = dt("w_in", [D, DIN], F32, kind="ExternalInput").ap()
    w_out = dt("w_out", [2 * D, D], F32, kind="ExternalInput").ap()
    normg_col_d = dt("normg_col", [128, 8], F32, kind="ExternalInput").ap()
    lng_col_d = dt("lng_col", [128, 8], F32, kind="ExternalInput").ap()
    gn_col_d = dt("gn_col", [128, 8], F32, kind="ExternalInput").ap()
    lnb_d = dt("lnb_bc", [128, D], F32, kind="ExternalInput").ap()
    bsb_d = dt("bs_bc", [128, D], F32, kind="ExternalInput").ap()
    lbp_d = dt("lbp", [128, 32], F32, kind="ExternalInput").ap()
    finalg_d = dt("finalg_bc", [128, D], F32, kind="ExternalInput").ap()
    ws_d = dt("ws_t", [128, NH * 128], F32, kind="ExternalInput").ap()
    ident_d = dt("c_ident", [128, 128], F32, kind="ExternalInput").ap()
    mf_d = dt("c_mf", [128, 128], F32, kind="ExternalInput").ap()
    mb_d = dt("c_mb", [128, 128], F32, kind="ExternalInput").ap()
    rmask_d = dt("c_rmask", [128, FL], F32, kind="ExternalInput").ap()
    onesm_d = dt("c_onesm", [128, 128], F32, kind="ExternalInput").ap()
    yseg = dt("yseg", [NSEG, T, D], F32, kind="ExternalOutput").ap()
    w_in_bf = dt("w_in_bf", [D, DIN], BF16, kind="Internal").ap()
    w_out_bf = dt("w_out_bf", [2 * D, D], BF16, kind="Internal").ap()

    stack = ExitStack()
    ARENA_BYTES = 206 * 1024
    arena_t = stack.enter_context(nc.sbuf_tensor("arena", [128, ARENA_BYTES], U8))
    ps_t = stack.enter_context(nc.psum_tensor("ps", [128, 8, 512], F32))
    AR = Arena(arena_t, ARENA_BYTES)
    P = Prog(nc, stack)
    PB = [P.buf("psum", i) for i in range(8)]

    def ps1(i):
        return ps_t[:, i, :]

    def ps2(i):
        return ps_t[:, i:i + 2, :].rearrange("p a b -> p (a b)")

    def ps1_bf(i):
        return ps_t[:, i, :].bitcast(BF16)

    ident_f = AR.alloc(512, F32)
    ident_b = AR.alloc(256, BF16)
    mf = AR.alloc(512, F32)
    mb = AR.alloc(512, F32)
    onesm = AR.alloc(512, F32)
    rmask = AR.alloc(FL * 4, F32)
    normg_col = AR.alloc(32, F32)
    lng_col = AR.alloc(32, F32)
    gn_col = AR.alloc(32, F32)
    lbp = AR.alloc(128, F32)
    lbe = AR.alloc(128, F32)
    lbv = AR.alloc(64, F32)
    lbden = AR.alloc(64, F32)
    sc_col = AR.alloc(64, F32)
    bi_col = AR.alloc(64, F32)
    nsc_col = AR.alloc(64, F32)
    lnsc_col = AR.alloc(64, F32)
    finalg = AR.alloc(4096, F32)
    wsT = AR.alloc(2048, BF16)
    cst = AR.alloc(4096, F32)
    wout_sb = AR.alloc(16 * 1024 * 2, BF16)
    xin = [AR.alloc(4096, F32) for _ in range(2)]
    mixa = AR.alloc(8 * T * 2, BF16)
    vtok = AR.alloc(NBLK * 1024 * 2, BF16)
    gateb = AR.alloc(8 * T * 2, BF16)
    ktT = [AR.alloc(8 * FL * 2, BF16) for _ in range(2)]
    qt = [AR.alloc(8 * T * 2, BF16) for _ in range(2)]
    dend = [AR.alloc(NH * NCH * 4, F32) for _ in range(2)]
    stat = AR.alloc(64 * 4, F32)
    xs6 = AR.alloc(NBLK * 2048, BF16)
    cpow = AR.alloc(8, F32)
    base_mark = AR.mark()

    bs_bc = arena_t[:, base_mark:base_mark + 4096].bitcast(F32)
    hT = AR.alloc(8 * TT * 2, BF16)
    wbuf = [AR.alloc(8 * 512 * 2, BF16) for _ in range(NWB)]
    th = AR.alloc(NTH * 4 * FL * 4, F32)
    blk_mark = AR.mark()
    junk = AR.alloc(2048, BF16)
    sz = [AR.alloc(2048, F32) for _ in range(NSZ)]
    xhat = [AR.alloc(2048, BF16) for _ in range(NXH)]
    tA = AR.alloc(4096, F32)
    blk_end = AR.mark()
    if ALIAS_1B:
        AR.reset(blk_mark)
    gbuf = [AR.alloc(FL * 4, F32) for _ in range(2)]
    kbuf = [AR.alloc(FL * 4, F32) for _ in range(2)]
    bbuf = [AR.alloc(FL * 4, F32) for _ in range(2)]
    ebuf = [AR.alloc(FL * 4, F32) for _ in range(2)]
    AR.reset(max(AR.mark(), blk_end))
    epbuf = [AR.alloc(FL * 4, F32) for _ in range(2)]
    x_peak = AR.mark()
    AR.reset(base_mark)
    shad = [[AR.alloc(2048, BF16) for _ in range(8)] for _ in range(2)]
    y_mark = AR.mark()
    kttok = [AR.alloc(5 * 1024 * 2, BF16) for _ in range(2)]
    smast = [AR.alloc(4096, F32) for _ in range(2)]
    stmp = [AR.alloc(4096, F32) for _ in range(2)]
    y2_peak = AR.mark()
    AR.reset(y_mark)
    t1 = AR.alloc(4096, F32)
    t2 = AR.alloc(4096, F32)
    scT = AR.alloc(2048, BF16)
    sq = AR.alloc(4096, F32)
    rstd_o = AR.alloc(4096, F32)
    t3 = AR.alloc(4096, F32)
    mixb = AR.alloc(2048, BF16)
    rbuf = AR.alloc(4096, F32)
    ybuf = [AR.alloc(4096, F32) for _ in range(2)]
    junk_y = AR.alloc(2048, BF16)
    y3_peak = AR.mark()
    print("arena peaks", x_peak, y2_peak, y3_peak, "of", ARENA_BYTES)

    def _rg(b, ap):
        for (o, req, dty, r) in AR.log:
            if r is ap:
                P.set_range(b, id(ap), o, o + (req + 63) // 64 * 64)
                return
        raise KeyError(b.name)
    _rg(P.buf("hT"), hT)
    for i in range(NWB):
        _rg(P.buf("wbuf", i), wbuf[i])
    for i in range(NTH * 4):
        _rg(P.buf("th", i), th)
    _rg(P.buf("junk"), junk)
    for i in range(NSZ):
        _rg(P.buf("sz", i), sz[i])
    for i in range(NXH):
        _rg(P.buf("xhat", i), xhat[i])
    _rg(P.buf("tA"), tA)
    for i in range(2):
        _rg(P.buf("gbuf", i), gbuf[i])
        _rg(P.buf("kbuf", i), kbuf[i])
        _rg(P.buf("bbuf", i), bbuf[i])
        _rg(P.buf("ebuf", i), ebuf[i])
        _rg(P.buf("epbuf", i), epbuf[i])
    for d_ in range(2):
        for j_ in range(8):
            _rg(P.buf("shad", d_, j_), shad[d_][j_])
        for l_ in range(5):
            _rg(P.buf("kttok", d_, l_), kttok[d_])
        _rg(P.buf("smast", d_), smast[d_])
        _rg(P.buf("ybuf", d_), ybuf[d_])
    for d_ in range(2):
        _rg(P.buf("stmp", d_), stmp[d_])
    for nm_, ap_ in (("t1", t1), ("t2", t2), ("scT", scT), ("sq", sq), ("rstd_o", rstd_o), ("t3", t3),
                     ("mixb", mixb), ("rbuf", rbuf), ("junk_y", junk_y)):
        _rg(P.buf(nm_), ap_)
    if dbg:
        def _flat(v):
            if isinstance(v, (list, tuple)):
                for i, x in enumerate(v):
                    for suf, y in _flat(x):
                        yield ("_%d" % i) + suf, y
            else:
                yield "", v
        for k, v in list(locals().items()):
            for suf, a in _flat(v):
                for (o, req, dty, r) in AR.log:
                    if r is a:
                        LAYOUT[k + suf] = (o, req, "bf16" if dty == BF16 else "f32")

    def v3(ap, a):
        return ap.rearrange("p (a b) -> p a b", a=a)

    def chk(name):
        if stage == name:
            P.stopped = True

    ld = P.dma_sem("ld")
    def dma_in(dst, src, bufs, sem, eng="sp"):
        P.op(eng, lambda e, dst=dst, src=src: e.dma_start(out=dst, in_=src), writes=bufs, dma=sem)

    cb = P.buf("consts")
    for dst, src in [(ident_f, ident_d), (mf, mf_d), (mb, mb_d), (onesm, onesm_d), (rmask, rmask_d),
                     (normg_col, normg_col_d), (lng_col, lng_col_d), (gn_col, gn_col_d), (lbp, lbp_d),
                     (finalg, finalg_d)]:
        dma_in(dst, src, [cb], ld)
    dma_in(tA, ws_d, [P.buf("tA")], P.dma_sem("ld_a"))
    tB = th[:, 0:1024]
    dma_in(tB, lnb_d, [P.buf("tB")], P.dma_sem("ld_b"))
    dma_in(bs_bc, bsb_d, [P.buf("bs_bc")], P.dma_sem("ld_s"))
    chk("s_ld")
    wcb = P.buf("wcast")
    for r in range(4):
        P.op("pool", lambda e, r=r: e.dma_start(out=w_out_bf[r * 512:(r + 1) * 512, :],
                                                in_=w_out[r * 512:(r + 1) * 512, :]),
             writes=[P.buf("wcast_o", r)], dma=P.dma_sem("wco%d" % r))
    for r in range(8):
        P.op("pool", lambda e, r=r: e.dma_start(out=w_in_bf[r * 128:(r + 1) * 128, :],
                                                in_=w_in[r * 128:(r + 1) * 128, :]),
             writes=[P.buf("wcast_i", r)], dma=P.dma_sem("wci%d" % r))
    chk("s_wc")
    cb2 = P.buf("consts2")
    P.op("dve", lambda e: e.tensor_copy(out=ident_b, in_=ident_f), reads=[cb], writes=[cb2])
    P.op("dve", lambda e: e.memset(cpow[:, 0:1], -0.5), writes=[P.buf("cpow")])
    P.op("dve", lambda e: e.memset(cpow[:, 1:2], EPS), writes=[P.buf("cpow")])

    def rsqrt_eps(src, dst, rbufs, wbuf_):
        P.op("pool", lambda e: e.tensor_tensor(out=dst, in0=src, in1=cpow[:, 1:2], op=ALU.add),
             reads=list(rbufs) + [P.buf("cpow")], writes=[wbuf_])
        P.op("pool", lambda e: e.tensor_tensor(out=dst, in0=dst, in1=cpow[:, 0:1], op=ALU.pow),
             reads=[wbuf_, P.buf("cpow")], writes=[wbuf_])
    P.op("act", lambda e: e.activation(out=lbe, in_=lbp, func=AF.Exp), reads=[cb], writes=[P.buf("lbe")])
    lbe4 = lbe.rearrange("p (d l h) -> p d l h", d=2, l=2)
    lb3 = lbv.rearrange("p (d h) -> p d h", d=2)
    lbden3 = lbden.rearrange("p (d h) -> p d h", d=2)
    P.op("dve", lambda e: e.tensor_tensor(out=lbden3, in0=lbe4[:, :, 0, :], in1=lbe4[:, :, 1, :], op=ALU.add),
         reads=[P.buf("lbe")], writes=[P.buf("lbden")])
    P.op("dve", lambda e: e.reciprocal(out=lbden, in_=lbden), reads=[P.buf("lbden")], writes=[P.buf("lbden")])
    P.op("dve", lambda e: e.tensor_tensor(out=lb3, in0=lbe4[:, :, 0, :], in1=lbden3, op=ALU.mult),
         reads=[P.buf("lbe"), P.buf("lbden")], writes=[P.buf("lbv")])
    lbB = P.buf("lbcols")
    P.op("dve", lambda e: e.tensor_scalar(out=sc_col, in0=lbv, scalar1=-0.5, scalar2=0.5, op0=ALU.mult, op1=ALU.add),
         reads=[P.buf("lbv")], writes=[lbB])
    P.op("dve", lambda e: e.tensor_scalar(out=bi_col, in0=lbv, scalar1=0.5, scalar2=0.5, op0=ALU.mult, op1=ALU.add),
         reads=[P.buf("lbv")], writes=[lbB])
    P.op("dve", lambda e: e.tensor_scalar(out=nsc_col, in0=lbv, scalar1=0.5, scalar2=-0.5, op0=ALU.mult, op1=ALU.add),
         reads=[P.buf("lbv")], writes=[lbB])
    P.op("act", lambda e: e.activation(out=lnsc_col, in_=sc_col, func=AF.Ln), reads=[lbB], writes=[lbB])
    chk("s_small")
    wsT3 = v3(wsT, 8)
    wsb = P.buf("wsT")
    tA3 = v3(tA, 8)
    P.op("dve", lambda e: e.tensor_copy(out=wsT, in_=tA), reads=[P.buf("tA")], writes=[wsb])

    def cst_mm(e):
        ins = None
        for h in range(NH):
            ins = e.matmul(ps2(0)[:, h * 128:(h + 1) * 128], lhsT=tB[:, h * 128:(h + 1) * 128], rhs=tA3[:, h, :],
                           start=True, stop=True)
        return ins
    P.op("pe", cst_mm, reads=[P.buf("tA"), P.buf("tB")], writes=[PB[0], PB[1]])
    P.op("dve", lambda e: e.tensor_tensor(out=cst, in0=ps2(0), in1=bs_bc, op=ALU.add),
         reads=[PB[0], PB[1], P.buf("bs_bc")], writes=[P.buf("cst")])
    chk("s_ws")
    wob = P.buf("wout")
    wo3 = v3(wout_sb, 16)
    ld_w = P.dma_sem("ld_w")
    for q4 in range(4):
        P.op("sp", lambda e, q4=q4: e.dma_start(
            out=wo3[:, q4 * 4:(q4 + 1) * 4, :],
            in_=w_out_bf[q4 * 512:(q4 + 1) * 512, :].rearrange("(c p) d -> p c d", p=128)),
            reads=[P.buf("wcast_o", q4)], writes=[P.buf("wout", q4)], dma=P.dma_sem("ld_w%d" % q4))
    P.barrier()

    xld = [P.dma_sem("x0"), P.dma_sem("x1")]
    wld = [P.dma_sem("w%d" % i) for i in range(NWB)]
    yst = [P.dma_sem("y0"), P.dma_sem("y1")]
    w_in_v = w_in_bf.rearrange("(dc p) c -> p dc c", p=128)
    hT3 = v3(hT, 8)
    statB = P.buf("stat")

    rr = dict(ps=0, w=0)

    def load_w(col0):
        slot = rr["w"] % NWB
        rr["w"] += 1
        P.op("sp", lambda e, slot=slot, col0=col0: e.dma_start(out=v3(wbuf[slot], 8), in_=w_in_v[:, :, col0:col0 + 512]),
             reads=[wcb], writes=[P.buf("wbuf", slot)], dma=wld[slot])
        return slot

    def fm_matmul(bank, slot, tcol, tok0, ntok, extra_reads=()):
        def fn(e):
            ins = None
            w3 = v3(wbuf[slot], 8)
            for dc in range(8):
                done = 0
                while done < ntok:
                    n = min(512, ntok - done)
                    ins = e.matmul(ps_t[:, bank + done // 512, 0:n], lhsT=w3[:, dc, tcol * 128:(tcol + 1) * 128],
                                   rhs=hT3[:, dc, tok0 + done: tok0 + done + n], start=(dc == 0), stop=(dc == 7))
                    done += n
            return ins
        wr = [PB[bank]] + ([PB[bank + 1]] if ntok > 512 else [])
        P.op("pe", fn, reads=[P.buf("wbuf", slot), P.buf("hT")] + list(extra_reads), writes=wr)
        return wr

    def tm_matmul(bank, slots, blk):
        def fn(e):
            ins = None
            for half in range(2):
                w3 = v3(wbuf[slots[half]], 8)
                for dc in range(8):
                    ins = e.matmul(ps_t[:, bank + half, :], lhsT=hT3[:, dc, blk * 128:(blk + 1) * 128],
                                   rhs=w3[:, dc, :], start=(dc == 0), stop=(dc == 7))
            return ins
        P.op("pe", fn, reads=[P.buf("wbuf", slots[0]), P.buf("wbuf", slots[1]), P.buf("hT")],
             writes=[PB[bank], PB[bank + 1]])


    xs63 = v3(xs6, NBLK)

    def emit_xnorm(seg):
        for blk in range(NBLK):
            sl = blk % 2
            P.op("sp", lambda e, sl=sl, blk=blk, seg=seg: e.dma_start(out=xin[sl], in_=xseg[seg, blk * 128:(blk + 1) * 128, :]),
                 writes=[P.buf("xin", sl)], dma=xld[sl])
            P.op("act", lambda e, sl=sl: e.activation(out=junk_y, in_=xin[sl], func=AF.Square, scale=1.0 / 32.0,
                                                      accum_out=stat[:, 0:1]),
                 reads=[P.buf("xin", sl)], writes=[P.buf("junk_y"), statB])
            rsqrt_eps(stat[:, 0:1], stat[:, 1:2], [statB], P.buf("stat1"))
            P.op("dve", lambda e, sl=sl, blk=blk: e.tensor_scalar(out=xs63[:, blk, :], in0=xin[sl], scalar1=stat[:, 1:2],
                                                                  scalar2=None, op0=ALU.mult),
                 reads=[P.buf("xin", sl), P.buf("stat1")], writes=[P.buf("xs", blk)])

    try:
      chk("setup")
      emit_xnorm(0)
      P.barrier()
      for seg in range(NSEG):
          P.cur_seg = seg
          for blk in range(NBLK):
              bk = blk % 2

              def tr(e, blk=blk, bk=bk):
                  ins = None
                  pv = v3(ps1_bf(bk), 8)
                  for dc in range(8):
                      ins = e.transpose(out=pv[:, dc, :], in_=xs63[:, blk, dc * 128:(dc + 1) * 128], identity=ident_b)
                  return ins
              P.op("pe", tr, reads=[P.buf("xs", blk), cb2], writes=[PB[bk]])
              P.op("dve", lambda e, bk=bk, blk=blk: e.tensor_tensor(
                  out=hT3[:, :, blk * 128:(blk + 1) * 128], in0=v3(ps1_bf(bk), 8),
                  in1=normg_col.unsqueeze(2).to_broadcast([128, 8, 128]), op=ALU.mult),
                  reads=[PB[bk], cb], writes=[P.buf("hT")])

          chk('p0')
          mixa3 = v3(mixa, 8)
          gateb3 = v3(gateb, 8)
          qs3 = v3(qt[1], 8)
          ktT3 = [v3(ktT[0], 8), v3(ktT[1], 8)]
          qt3 = [v3(qt[0], 8), v3(qt[1], 8)]
          dend3 = [v3(dend[0], 8), v3(dend[1], 8)]
          th3 = v3(th, NTH * 4)
          vtok3 = v3(vtok, NBLK)
          def part_uz():
              for half in range(2):
                  su = load_w(0 + half * 512)
                  szl = load_w(2048 + half * 512)
                  for tcol in range(4):
                      ft = half * 4 + tcol
                      bu = (2 * tcol) % 4
                      bz = bu + 1
                      fm_matmul(bu, su, tcol, HB, T)
                      fm_matmul(bz, szl, tcol, HB, T)
                      s2 = ft % NSZ
                      P.op("act", lambda e, bz=bz, s2=s2: e.activation(out=sz[s2][:, 0:512], in_=ps1(bz), func=AF.Silu),
                           reads=[PB[bz]], writes=[P.buf("sz", s2)])
                      P.op("dve", lambda e, bu=bu, s2=s2, ft=ft: e.tensor_tensor(out=mixa3[:, ft, :], in0=ps1(bu), in1=sz[s2][:, 0:512],
                                                                                op=ALU.mult),
                           reads=[PB[bu], P.buf("sz", s2)], writes=[P.buf("mixa", ft)])

          def part_va():
              s0 = load_w(1024)
              s1 = load_w(1536)
              for b in range(NMB):
                  blk = 1 + b
                  bank = 4 if b % 2 == 0 else 6
                  tm_matmul(bank, (s0, s1), blk)
                  P.op("act", lambda e, bank=bank: e.activation(out=junk, in_=ps2(bank), func=AF.Identity, scale=1.0 / 1024.0,
                                                                accum_out=stat[:, 2:3]),
                       reads=[PB[bank], PB[bank + 1]], writes=[P.buf("junk"), P.buf("stat2")])
                  P.op("act", lambda e, bank=bank: e.activation(out=junk, in_=ps2(bank), func=AF.Square, scale=1.0 / 32.0,
                                                                accum_out=stat[:, 3:4]),
                       reads=[PB[bank], PB[bank + 1]], writes=[P.buf("junk"), P.buf("stat3")])
                  P.op("dve", lambda e: e.tensor_tensor(out=stat[:, 4:5], in0=stat[:, 2:3], in1=stat[:, 2:3], op=ALU.mult),
                       reads=[P.buf("stat2")], writes=[P.buf("stat4")])
                  P.op("dve", lambda e: e.tensor_tensor(out=stat[:, 5:6], in0=stat[:, 3:4], in1=stat[:, 4:5], op=ALU.subtract),
                       reads=[P.buf("stat3"), P.buf("stat4")], writes=[P.buf("stat5")])
                  rsqrt_eps(stat[:, 5:6], stat[:, 6:7], [P.buf("stat5")], P.buf("stat6"))
                  P.op("dve", lambda e: e.scalar_tensor_tensor(out=stat[:, 7:8], in0=stat[:, 2:3], scalar=-1.0, in1=stat[:, 6:7],
                                                               op0=ALU.mult, op1=ALU.mult),
                       reads=[P.buf("stat2"), P.buf("stat6")], writes=[P.buf("stat7")])
                  xsl = b % NXH
                  P.op("act", lambda e, bank=bank, xsl=xsl: e.activation(out=xhat[xsl], in_=ps2(bank), func=AF.Identity,
                                                                         scale=stat[:, 6:7], bias=stat[:, 7:8]),
                       reads=[PB[bank], PB[bank + 1], P.buf("stat6"), P.buf("stat7")], writes=[P.buf("xhat", xsl)])
                  sb = 0 if b % 2 == 0 else 2

                  def spat(e, xsl=xsl, sb=sb):
                      ins = None
                      for h in range(NH):
                          ins = e.matmul(ps2(sb)[:, h * 128:(h + 1) * 128], lhsT=xhat[xsl][:, h * 128:(h + 1) * 128],
                                         rhs=wsT3[:, h, :], start=True, stop=True)
                      return ins
                  P.op("pe", spat, reads=[P.buf("xhat", xsl), wsb], writes=[PB[sb], PB[sb + 1]])
                  P.op("dve", lambda e, sb=sb: e.tensor_tensor(out=v3(tA, 8), in0=v3(ps2(sb), 8),
                                                               in1=lng_col.unsqueeze(2).to_broadcast([128, 8, 128]), op=ALU.mult),
                       reads=[PB[sb], PB[sb + 1], cb], writes=[P.buf("tA")])
                  P.op("dve", lambda e: e.tensor_tensor(out=tA, in0=tA, in1=cst, op=ALU.add),
                       reads=[P.buf("tA"), P.buf("cst")], writes=[P.buf("tA")])
                  P.op("dve", lambda e, b=b: e.tensor_tensor(out=mixa3[:, :, b * 128:(b + 1) * 128], in0=v3(tA, 8),
                                                             in1=mixa3[:, :, b * 128:(b + 1) * 128], op=ALU.mult),
                       reads=[P.buf("tA")] + [P.buf("mixa", ft) for ft in range(8)],
                       writes=[P.buf("mixa", ft) for ft in range(8)])

          def part_q():
              for half in range(2):
                  sq_ = load_w(3072 + half * 512)
                  for tcol in range(4):
                      ft = half * 4 + tcol
                      bk = tcol % 4
                      fm_matmul(bk, sq_, tcol, HB, T)
                      P.op("act", lambda e, bk=bk, ft=ft: e.activation(out=qs3[:, ft, :], in_=ps1(bk), func=AF.Silu),
                           reads=[PB[bk]], writes=[P.buf("qt1", ft)])

          def part_zb():
              for half in range(2):
                  szb = load_w(7168 + half * 512)
                  for tcol in range(4):
                      ft = half * 4 + tcol
                      bk = tcol % 4
                      fm_matmul(bk, szb, tcol, HB, T)
                      s2 = ft % NSZ
                      P.op("act", lambda e, bk=bk, s2=s2: e.activation(out=sz[s2][:, 0:512], in_=ps1(bk), func=AF.Silu),
                           reads=[PB[bk]], writes=[P.buf("sz", s2)])
                      P.op("dve", lambda e, s2=s2, ft=ft: e.tensor_scalar(out=gateb3[:, ft, :], in0=sz[s2][:, 0:512],
                                                                          scalar1=gn_col[:, ft:ft + 1], scalar2=None, op0=ALU.mult),
                           reads=[P.buf("sz", s2), cb], writes=[P.buf("gateb", ft)])

          def part_i():
              s0 = load_w(6144)
              s1 = load_w(6656)
              for blk in range(NBLK):
                  bank = 4 if blk % 2 == 0 else 6
                  tm_matmul(bank, (s0, s1), blk)
                  P.op("act", lambda e, bank=bank, blk=blk: e.activation(out=vtok3[:, blk, :], in_=ps2(bank), func=AF.Copy),
                       reads=[PB[bank], PB[bank + 1]], writes=[P.buf("vtok", blk)])


          def part_f():
              if ALIAS_1B and USE_BARRIERS:
                  P.handoff([P.buf("junk"), P.buf("sz", 0), P.buf("sz", 1), P.buf("xhat", 0), P.buf("xhat", 1), P.buf("tA")],
                            [P.buf(nm, i) for nm in ("gbuf", "kbuf", "bbuf", "ebuf") for i in range(2)])
              for d in range(2):
                  tok0 = 0 if d == 0 else HB
                  moff = HB if d == 0 else 0
                  for half in range(2):
                      sw = load_w(4096 + d * 1024 + half * 512)
                      tb_ = ((d * 2 + half) % NTH) * 4
                      for tcol in range(4):
                          bank = 4 if tcol % 2 == 0 else 6
                          fm_matmul(bank, sw, tcol, tok0, FL)
                          P.op("act", lambda e, bank=bank, ti=tb_ + tcol: e.activation(out=th3[:, ti, :], in_=ps2(bank)[:, 0:FL],
                                                                                  func=AF.Tanh, scale=-0.5),
                               reads=[PB[bank], PB[bank + 1]], writes=[P.buf("th", tb_ + tcol)])
                      for tcol in range(4):
                          ft = half * 4 + tcol
                          s2 = tcol % 2
                          ci = d * 8 + ft
                          P.op("act", lambda e, ti=tb_ + tcol, s2=s2, ci=ci: e.activation(
                              out=gbuf[s2], in_=th3[:, ti, :], func=AF.Ln, scale=nsc_col[:, ci:ci + 1], bias=bi_col[:, ci:ci + 1]),
                              reads=[P.buf("th", tb_ + tcol), lbB], writes=[P.buf("gbuf", s2)])
                          if d == 0:
                              P.op("dve", lambda e, s2=s2: e.tensor_tensor_scan(out=bbuf[s2], data0=rmask, data1=gbuf[s2], initial=0.0,
                                                                                op0=ALU.mult, op1=ALU.add),
                                   reads=[P.buf("gbuf", s2), cb], writes=[P.buf("bbuf", s2)])
                          else:
                              P.op("dve", lambda e, s2=s2: e.tensor_tensor_scan(out=bbuf[s2][:, ::-1], data0=rmask,
                                                                                data1=gbuf[s2][:, ::-1], initial=0.0,
                                                                                op0=ALU.mult, op1=ALU.add),
                                   reads=[P.buf("gbuf", s2), cb], writes=[P.buf("bbuf", s2)])
                          bsrc, bname = bbuf, "bbuf"
                          P.op("act", lambda e, s2=s2, bsrc=bsrc, ci=ci: e.activation(out=ebuf[s2], in_=bsrc[s2], func=AF.Exp, scale=-1.0,
                                                                                     bias=lnsc_col[:, ci:ci + 1]),
                               reads=[P.buf(bname, s2), lbB], writes=[P.buf("ebuf", s2)])
                          P.op("dve", lambda e, s2=s2, d=d, ft=ft, ti=tb_ + tcol: e.scalar_tensor_tensor(
                              out=ktT3[d][:, ft, :], in0=th3[:, ti, :], scalar=1.0, in1=ebuf[s2], op0=ALU.add, op1=ALU.mult),
                               reads=[P.buf("th", tb_ + tcol), P.buf("ebuf", s2)], writes=[P.buf("ktT", d, ft)])
                          P.op("act", lambda e, s2=s2, bsrc=bsrc: e.activation(out=epbuf[s2], in_=bsrc[s2], func=AF.Exp),
                               reads=[P.buf(bname, s2)], writes=[P.buf("epbuf", s2)])
                          P.op("pool", lambda e, s2=s2, d=d, ft=ft, moff=moff: e.tensor_tensor(
                              out=qt3[d][:, ft, :], in0=qs3[:, ft, :], in1=epbuf[s2][:, moff:moff + T], op=ALU.mult),
                              reads=[P.buf("qt1", ft), P.buf("epbuf", s2)],
                              writes=[P.buf("qt%d" % d, ft)])
                          cpos = 63 if d == 0 else 0
                          P.op("dve", lambda e, s2=s2, d=d, ft=ft, cpos=cpos: e.tensor_copy(
                              out=dend3[d][:, ft, :], in_=v3(epbuf[s2], NCH)[:, :, cpos]),
                              reads=[P.buf("epbuf", s2)], writes=[P.buf("dend", d)])


          for _pn in PART_ORDER:
              {'uz': part_uz, 'va': part_va, 'q': part_q, 'zb': part_zb, 'i': part_i, 'f': part_f}[_pn]()
          chk('p1')
          if USE_BARRIERS:
              P.barrier()
          for d in range(2):
              kt3 = v3(kttok[d], 5)
              for lb_ in range(5):
                  bk = lb_ % 2

                  def trk(e, d=d, lb_=lb_, bk=bk):
                      ins = None
                      pv = v3(ps1_bf(bk), 8)
                      for ft in range(8):
                          ins = e.transpose(out=pv[:, ft, :], in_=ktT3[d][:, ft, lb_ * 128:(lb_ + 1) * 128], identity=ident_b)
                      return ins
                  P.op("pe", trk, reads=[P.buf("ktT", d, ft) for ft in range(8)] + [cb2], writes=[PB[bk]])
                  P.op("act", lambda e, bk=bk, lb_=lb_, kt3=kt3: e.activation(out=kt3[:, lb_, :], in_=ps1_bf(bk), func=AF.Copy),
                       reads=[PB[bk]], writes=[P.buf("kttok", d, lb_)])
              gb0 = 0 if d == 0 else 1
              order = list(range(0, NCH - 1)) if d == 0 else list(range(NCH - 1, 0, -1))
              first = True
              for n, c in enumerate(order):
                  bank = 2 + 2 * (n % 3)
                  lb_ = c // 2
                  p0 = (c % 2) * 64

                  def pm(e, d=d, lb_=lb_, p0=p0, bank=bank, kt3=kt3, gb0=gb0):
                      ins = None
                      for h in range(NH):
                          ins = e.matmul(ps2(bank)[:, h * 128:(h + 1) * 128],
                                         lhsT=kt3[p0:p0 + 64, lb_, h * 128:(h + 1) * 128],
                                         rhs=vtok3[p0:p0 + 64, gb0 + lb_, h * 128:(h + 1) * 128], start=True, stop=True)
                      return ins
                  P.op("pe", pm, reads=[P.buf("kttok", d, lb_), P.buf("vtok", gb0 + lb_)], writes=[PB[bank], PB[bank + 1]])
                  dbc = dend3[d][:, :, c:c + 1].to_broadcast([128, 8, 128])
                  if first:
                      P.op("dve", lambda e, bank=bank, d=d, dbc=dbc: e.tensor_tensor(out=v3(smast[d], 8), in0=v3(ps2(bank), 8),
                                                                                    in1=dbc, op=ALU.mult),
                           reads=[PB[bank], PB[bank + 1], P.buf("dend", d)], writes=[P.buf("smast", d)])
                      first = False
                  else:
                      P.op("dve", lambda e, bank=bank, d=d: e.tensor_tensor(out=stmp[d], in0=ps2(bank), in1=smast[d], op=ALU.add),
                           reads=[PB[bank], PB[bank + 1], P.buf("smast", d)], writes=[P.buf("stmp", d)])
                      P.op(MULT_ENG[d], lambda e, d=d, dbc=dbc: e.tensor_tensor(out=v3(smast[d], 8), in0=v3(stmp[d], 8), in1=dbc,
                                                                                 op=ALU.mult),
                           reads=[P.buf("stmp", d), P.buf("dend", d)], writes=[P.buf("smast", d)])
                  tgt = c + 1 if d == 0 else c - 1
                  j = tgt - 2 if d == 0 else tgt
                  if 0 <= j < 8:
                      P.op("act", lambda e, d=d, j=j: e.activation(out=shad[d][j], in_=smast[d], func=AF.Copy),
                           reads=[P.buf("smast", d)], writes=[P.buf("shad", d, j)])
          chk('p2')
          if USE_BARRIERS:
              P.barrier()
          if seg + 1 < NSEG:
              emit_xnorm(seg + 1)
          for b in range(NMB):
              offs = [HB + b * 128, b * 128]
              for d in range(2):
                  bank = 0 if d == 0 else 2

                  def scm(e, d=d, bank=bank, b=b, offs=offs):
                      ins = None
                      for h in range(NH):
                          ins = e.matmul(ps2(bank)[:, h * 128:(h + 1) * 128],
                                         lhsT=ktT3[d][:, h, offs[d]:offs[d] + 128],
                                         rhs=qt3[d][:, h, b * 128:(b + 1) * 128], start=True, stop=True)
                      return ins
                  P.op("pe", scm, reads=[P.buf("ktT", d, ft) for ft in range(8)] + [P.buf("qt%d" % d, ft) for ft in range(8)],
                       writes=[PB[bank], PB[bank + 1]])
              P.op("dve", lambda e: e.tensor_tensor(out=v3(t1, 8), in0=v3(ps2(0), 8),
                                                    in1=mf.unsqueeze(1).to_broadcast([128, 8, 128]), op=ALU.mult),
                   reads=[PB[0], PB[1], cb], writes=[P.buf("t1")])
              P.op("dve", lambda e: e.tensor_tensor(out=v3(t2, 8), in0=v3(ps2(2), 8),
                                                    in1=mb.unsqueeze(1).to_broadcast([128, 8, 128]), op=ALU.mult),
                   reads=[PB[2], PB[3], cb], writes=[P.buf("t2")])
              P.op("dve", lambda e: e.tensor_tensor(out=scT, in0=t1, in1=t2, op=ALU.add),
                   reads=[P.buf("t1"), P.buf("t2")], writes=[P.buf("scT")])
              scT3 = v3(scT, 8)

              def om(e, b=b):
                  ins = None
                  for h in range(NH):
                      hs = slice(h * 128, (h + 1) * 128)
                      for hf in range(2):
                          j = 2 * b + hf
                          cs = slice(h * 128 + hf * 64, h * 128 + hf * 64 + 64)
                          ts = slice(b * 128 + hf * 64, b * 128 + hf * 64 + 64)
                          e.matmul(ps2(4)[:, cs], lhsT=vtok3[:, 1 + b, hs], rhs=scT3[:, h, hf * 64:(hf + 1) * 64],
                                   start=True, stop=False)
                          e.matmul(ps2(4)[:, cs], lhsT=shad[0][j][:, hs], rhs=qt3[0][:, h, ts], start=False, stop=False)
                          ins = e.matmul(ps2(4)[:, cs], lhsT=shad[1][j][:, hs], rhs=qt3[1][:, h, ts], start=False, stop=True)
                  return ins
              P.op("pe", om, reads=[P.buf("vtok", 1 + b), P.buf("scT")] + [P.buf("shad", d, 2 * b + hf) for d in range(2) for hf in range(2)]
                   + [P.buf("qt%d" % d, ft) for d in range(2) for ft in range(8)], writes=[PB[4], PB[5]])
              P.op("act", lambda e: e.activation(out=sq, in_=ps2(4), func=AF.Square), reads=[PB[4], PB[5]], writes=[P.buf("sq")])

              def ssm(e):
                  e.matmul(ps_t[:, 6, :], lhsT=onesm, rhs=sq[:, 0:512], start=True, stop=True)
                  return e.matmul(ps_t[:, 7, :], lhsT=onesm, rhs=sq[:, 512:1024], start=True, stop=True)
              P.op("pe", ssm, reads=[P.buf("sq"), cb], writes=[PB[6], PB[7]])
              P.op("act", lambda e: e.activation(out=rstd_o, in_=ps2(6), func=AF.Ln, bias=cpow[:, 1:2]),
                   reads=[PB[6], PB[7], P.buf("cpow")], writes=[P.buf("rstd_o")])
              P.op("act", lambda e: e.activation(out=rstd_o, in_=rstd_o, func=AF.Exp, scale=-0.5),
                   reads=[P.buf("rstd_o")], writes=[P.buf("rstd_o")])
              P.op("dve", lambda e: e.tensor_tensor(out=t3, in0=ps2(4), in1=rstd_o, op=ALU.mult),
                   reads=[PB[4], PB[5], P.buf("rstd_o")], writes=[P.buf("t3")])
              P.op("dve", lambda e, b=b: e.tensor_tensor(out=v3(mixb, 8), in0=v3(t3, 8), in1=gateb3[:, :, b * 128:(b + 1) * 128],
                                                          op=ALU.mult),
                   reads=[P.buf("t3")] + [P.buf("gateb", ft) for ft in range(8)], writes=[P.buf("mixb")])
              mixb3 = v3(mixb, 8)

              def outp(e, b=b):
                  ins = None
                  for half in range(2):
                      for ec in range(16):
                          lhsT = mixa3[:, ec, b * 128:(b + 1) * 128] if ec < 8 else mixb3[:, ec - 8, :]
                          ins = e.matmul(ps_t[:, 6 + half, :], lhsT=lhsT, rhs=wo3[:, ec, half * 512:(half + 1) * 512],
                                         start=(ec == 0), stop=(ec == 15))
                  return ins
              P.op("pe", outp, reads=[P.buf("mixb"), wob] + [P.buf("mixa", ft) for ft in range(8)], writes=[PB[6], PB[7]])
              sl = b % 2
              P.op("sp", lambda e, sl=sl, b=b, seg=seg: e.dma_start(out=xin[sl], in_=xseg[seg, (1 + b) * 128:(2 + b) * 128, :]),
                   writes=[P.buf("xin", sl)], dma=xld[sl])
              P.op("dve", lambda e, sl=sl: e.tensor_tensor(out=rbuf, in0=ps2(6), in1=xin[sl], op=ALU.add),
                   reads=[PB[6], PB[7], P.buf("xin", sl)], writes=[P.buf("rbuf")])
              P.op("act", lambda e: e.activation(out=junk_y, in_=rbuf, func=AF.Square, scale=1.0 / 32.0, accum_out=stat[:, 8:9]),
                   reads=[P.buf("rbuf")], writes=[P.buf("junk_y"), P.buf("stat8")])
              rsqrt_eps(stat[:, 8:9], stat[:, 9:10], [P.buf("stat8")], P.buf("stat9"))
              P.op("dve", lambda e, sl=sl: e.scalar_tensor_tensor(out=ybuf[sl], in0=rbuf, scalar=stat[:, 9:10], in1=finalg,
                                                                  op0=ALU.mult, op1=ALU.mult),
                   reads=[P.buf("rbuf"), P.buf("stat9"), cb], writes=[P.buf("ybuf", sl)])
              P.op("pool", lambda e, sl=sl, b=b, seg=seg: e.dma_start(out=yseg[seg, b * 128:(b + 1) * 128, :], in_=ybuf[sl]),
                   reads=[P.buf("ybuf", sl)], dma=yst[sl])
          if USE_BARRIERS:
              P.barrier()
          chk('seg1')

    except _Stop:
        pass
    P.stopped = False
    if dbg:
        dbg_arena = dt("dbg_arena", [128, ARENA_BYTES], U8, kind="ExternalOutput").ap()
        dbg_psum = dt("dbg_psum", [128, 4096], F32, kind="ExternalOutput").ap()
        P.barrier()
        dsm = P.dma_sem("dbg")
        CH = ARENA_BYTES // 4
        for i in range(4):
            P.op("sp", lambda e, i=i: e.dma_start(out=dbg_arena[:, i * CH:(i + 1) * CH], in_=arena_t[:, i * CH:(i + 1) * CH]),
                 dma=dsm)
        P.barrier()
        pst = arena_t[:, 0:16384].bitcast(F32)
        P.op("dve", lambda e: e.tensor_copy(out=pst, in_=ps_t[:, :, :].rearrange("p a b -> p (a b)")), writes=[P.buf("pst")])
        P.op("sp", lambda e: e.dma_start(out=dbg_psum, in_=pst), reads=[P.buf("pst")], dma=dsm)
    P.finish()
    stack.close()
    return nc


def _segments():
    segs = []
    for b in range(4):
        for j in range(8192 // T):
            segs.append((0, b, j * T))
    for j in range(16384 // T):
        segs.append((1, 0, j * T))
    return segs


_NC_CACHE = {}


def _in_maps(x_prompt, x_sample, norm_g, w_in, ln_v_g, ln_v_b, w_s, b_s, lb_params, gn_g, w_out, final_g):
    f32 = np.float32
    xs_ = [np.asarray(x_prompt, f32), np.asarray(x_sample, f32)]
    segs = _segments()
    assert len(segs) == NCORES * NSEG
    xseg = np.zeros((NCORES, NSEG, TT, D), f32)
    for gi, (which, b, t0) in enumerate(segs):
        c, s = divmod(gi, NSEG)
        L = xs_[which].shape[1]
        lo, hi = t0 - HB, t0 + T + HB
        slo, shi = max(lo, 0), min(hi, L)
        xseg[c, s, slo - lo: shi - lo, :] = xs_[which][b, slo:shi, :]

    def col8(v):
        return np.ascontiguousarray(np.asarray(v, f32).reshape(8, 128).T)

    consts = {}
    consts["c_ident"] = np.eye(128, dtype=f32)
    si, ti = np.meshgrid(np.arange(128), np.arange(128), indexing="ij")
    same = (si // 64) == (ti // 64)
    consts["c_mf"] = (same & (si <= ti)).astype(f32)
    consts["c_mb"] = (same & (si >= ti)).astype(f32)
    rm = np.ones((128, FL), f32)
    rm[:, 0::64] = 0.0
    consts["c_rmask"] = rm
    consts["c_onesm"] = np.full((128, 128), 1.0 / 128.0, f32)

    shared = dict(
        w_in=np.ascontiguousarray(np.asarray(w_in, f32)[0]),
        w_out=np.ascontiguousarray(np.asarray(w_out, f32)[0]),
        normg_col=col8(norm_g[0]),
        lng_col=col8(ln_v_g[0]),
        gn_col=col8(gn_g[0]),
        lnb_bc=np.ascontiguousarray(np.broadcast_to(np.asarray(ln_v_b, f32).reshape(1, D), (128, D))),
        bs_bc=np.ascontiguousarray(np.broadcast_to(np.asarray(b_s, f32).reshape(1, D), (128, D))),
        lbp=np.ascontiguousarray(np.asarray(lb_params, f32).reshape(2, 2, 8, 128).transpose(3, 0, 1, 2).reshape(128, 32)),
        finalg_bc=np.ascontiguousarray(np.broadcast_to(np.asarray(final_g, f32).reshape(1, D), (128, D))),
        ws_t=np.ascontiguousarray(np.asarray(w_s, f32)[0].transpose(2, 0, 1).reshape(128, NH * 128)),
        **consts,
    )
    in_maps = []
    for c in range(NCORES):
        m = dict(shared)
        m["xseg"] = xseg[c]
        in_maps.append(m)
    return in_maps


def kernel(x_prompt, x_sample, norm_g, w_in, ln_v_g, ln_v_b, w_s, b_s, lb_params, gn_g, w_out, final_g):
    f32 = np.float32
    in_maps = _in_maps(x_prompt, x_sample, norm_g, w_in, ln_v_g, ln_v_b, w_s, b_s, lb_params, gn_g, w_out, final_g)
    segs = _segments()
    if "nc" not in _NC_CACHE:
        _NC_CACHE["nc"] = build_program()
    nc = _NC_CACHE["nc"]
    res = run_bass_kernel_spmd(nc, in_maps, core_ids=list(range(NCORES)))
    y_p = np.zeros((4, 8192, D), f32)
    y_s = np.zeros((1, 16384, D), f32)
    outs = [y_p, y_s]
    for gi, (which, b, t0) in enumerate(segs):
        c, s = divmod(gi, NSEG)
        outs[which][b, t0:t0 + T, :] = res.results[c]["yseg"][s]
    return (y_p, y_s)
```
